# Optimizing a Trainium2 kernel written in Bass

```python
import jax
import jax.numpy as jnp
from jax import lax
import numpy as np

D_MODEL = 1024
BATCH = 8
SEQ = 2048
DEPTH = 1

GRID_W = 64
CTX_LEN = 256
ATT_HEADS = 16
ATT_KV_HEADS = 4
ATT_HEAD_DIM = 64
ATT_GROUP = ATT_HEADS // ATT_KV_HEADS
WINDOW = 128
ATT_BLOCK = 128
ROPE_BASE = 10000.0
ATT_SCALE = ATT_HEAD_DIM ** -0.5
ML_HEADS = 4
ML_QK_DIM = 128
ML_V_DIM = 256
ML_CHUNK = 128
D_FF = 2816
CONV_W = 3
EPS = 1e-6
NEG_INF = -1e30

ATT_Q_W = ATT_HEADS * ATT_HEAD_DIM
ATT_KV_W = ATT_KV_HEADS * ATT_HEAD_DIM
ML_QK_W = ML_HEADS * ML_QK_DIM
ML_V_W = ML_HEADS * ML_V_DIM
ML_GATE_W = 2 * 2 * ML_HEADS
IN_SPLITS = (ATT_Q_W, ATT_KV_W, ATT_KV_W, ML_QK_W, ML_QK_W, ML_V_W, ML_V_W, ML_GATE_W, D_MODEL, D_MODEL)
IN_W = ATT_Q_W + 2 * ATT_KV_W + 2 * ML_QK_W + 2 * ML_V_W + ML_GATE_W + 2 * D_MODEL

kernel_name = "hybrid_mlstm_swa_dit_layer"


def rms_norm(x, w):
    xf = x.astype(jnp.float32)
    y = xf * lax.rsqrt(jnp.mean(xf * xf, axis=-1, keepdims=True) + EPS)
    return (y * w.astype(jnp.float32)).astype(x.dtype)


def modulate(h, shift, scale):
    return h * (1 + scale[:, None, :]) + shift[:, None, :]


def heads(a, n_heads, head_dim):
    return a.reshape(a.shape[0], a.shape[1], n_heads, head_dim)


def split_in_proj(p):
    parts, start = [], 0
    for width in IN_SPLITS:
        parts.append(p[..., start:start + width])
        start += width
    return parts


def axial_rope(rows):
    row = jnp.repeat(jnp.arange(rows, dtype=jnp.float32), GRID_W)
    col = jnp.tile(jnp.arange(GRID_W, dtype=jnp.float32), rows)
    n_freq = ATT_HEAD_DIM // 4
    inv_freq = ROPE_BASE ** (-jnp.arange(n_freq, dtype=jnp.float32) / n_freq)
    ang = jnp.concatenate([row[:, None] * inv_freq, col[:, None] * inv_freq], axis=-1)
    return jnp.cos(ang)[:, None, :], jnp.sin(ang)[:, None, :]


def apply_rope(x, cos, sin):
    xf = x.astype(jnp.float32)
    half = x.shape[-1] // 2
    x1, x2 = xf[..., :half], xf[..., half:]
    return jnp.concatenate([x1 * cos - x2 * sin, x1 * sin + x2 * cos], axis=-1).astype(x.dtype)


def windowed_gqa(q, k, v, k_ctx, v_ctx, sink):
    B, L, H, dh = q.shape
    T = ATT_BLOCK
    nb = L // T
    C = k_ctx.shape[1]
    qb = q.reshape(B, nb, T, ATT_KV_HEADS, ATT_GROUP, dh).swapaxes(0, 1)

    def band(a):
        ap = jnp.pad(a, ((0, 0), (T, T), (0, 0), (0, 0))).reshape(B, nb + 2, T, ATT_KV_HEADS, dh)
        w = jnp.concatenate([ap[:, :-2], ap[:, 1:-1], ap[:, 2:]], axis=2)
        return w.swapaxes(0, 1)

    kb, vb = band(k), band(v)
    sink_b = jnp.broadcast_to(sink.astype(jnp.float32).reshape(1, ATT_KV_HEADS, ATT_GROUP, 1, 1),
                              (B, ATT_KV_HEADS, ATT_GROUP, T, 1))
    key_off = jnp.arange(3 * T) - T
    rel = key_off[None, :] - jnp.arange(T)[:, None]

    def one_block(args):
        blk, qblk, kblk, vblk = args
        k_abs = blk * T + key_off
        valid = (jnp.abs(rel) <= WINDOW) & ((k_abs >= 0) & (k_abs < L))[None, :]
        s_lat = jnp.einsum('btkgd,bskd->bkgts', qblk, kblk).astype(jnp.float32) * ATT_SCALE
        s_lat = jnp.where(valid, s_lat, NEG_INF)
        s_ctx = jnp.einsum('btkgd,bskd->bkgts', qblk, k_ctx).astype(jnp.float32) * ATT_SCALE
        p = jax.nn.softmax(jnp.concatenate([s_lat, s_ctx, sink_b], axis=-1), axis=-1).astype(v.dtype)
        return (jnp.einsum('bkgts,bskd->btkgd', p[..., :3 * T], vblk)
                + jnp.einsum('bkgts,bskd->btkgd', p[..., 3 * T:3 * T + C], v_ctx))

    o = lax.map(one_block, (jnp.arange(nb), qb, kb, vb))
    return o.swapaxes(0, 1).reshape(B, L, H * dh)


def context_gqa(q_ctx, k_ctx, v_ctx, sink):
    B, C, H, dh = q_ctx.shape
    qg = q_ctx.reshape(B, C, ATT_KV_HEADS, ATT_GROUP, dh)
    s = jnp.einsum('btkgd,bskd->bkgts', qg, k_ctx).astype(jnp.float32) * ATT_SCALE
    sink_b = jnp.broadcast_to(sink.astype(jnp.float32).reshape(1, ATT_KV_HEADS, ATT_GROUP, 1, 1),
                              (B, ATT_KV_HEADS, ATT_GROUP, C, 1))
    p = jax.nn.softmax(jnp.concatenate([s, sink_b], axis=-1), axis=-1).astype(v_ctx.dtype)
    return jnp.einsum('bkgts,bskd->btkgd', p[..., :C], v_ctx).reshape(B, C, H * dh)


def mlstm_chunkwise(q, k, v, i_pre, f_pre, state):
    B, L, H, dqk = q.shape
    dv = v.shape[-1]
    T = ML_CHUNK
    nc = L // T

    def chunks(a):
        return a.astype(jnp.float32).reshape(B, nc, T, *a.shape[2:]).swapaxes(0, 1)

    qs, vs = chunks(q), chunks(v)
    ks = chunks(k) * (dqk ** -0.5)
    i_s = chunks(i_pre)
    lf_s = jax.nn.log_sigmoid(chunks(f_pre))
    tril = jnp.tril(jnp.ones((T, T), dtype=bool))

    def step(carry, xs):
        Cm, n, m = carry
        qc, kc, vc, ic, lfc = xs
        b = jnp.cumsum(lfc, axis=1).transpose(0, 2, 1)
        ih = ic.transpose(0, 2, 1)
        dmat = jnp.where(tril, b[..., :, None] - b[..., None, :] + ih[..., None, :], -jnp.inf)
        inter = b + m[..., None]
        m_t = jnp.maximum(inter, dmat.max(axis=-1))
        s = jnp.einsum('bthd,bshd->bhts', qc, kc) * jnp.exp(dmat - m_t[..., None])
        w_inter = jnp.exp(inter - m_t)
        num = (jnp.einsum('bhts,bshv->bthv', s, vc)
               + jnp.einsum('bthd,bhdv->bthv', qc, Cm) * w_inter.transpose(0, 2, 1)[..., None])
        nq = s.sum(axis=-1) + w_inter * jnp.einsum('bthd,bhd->bht', qc, n)
        den = jnp.maximum(jnp.abs(nq), jnp.exp(-m_t))
        h = num / den.transpose(0, 2, 1)[..., None]
        b_last = b[..., -1]
        a = b_last[..., None] - b + ih
        m_new = jnp.maximum(b_last + m, a.max(axis=-1))
        wk = jnp.exp(a - m_new[..., None]).transpose(0, 2, 1)[..., None]
        decay = jnp.exp(b_last + m - m_new)
        C_new = decay[..., None, None] * Cm + jnp.einsum('bshd,bshv->bhdv', kc * wk, vc)
        n_new = decay[..., None] * n + jnp.einsum('bshd->bhd', kc * wk)
        return (C_new, n_new, m_new), h

    state, hs = lax.scan(step, state, (qs, ks, vs, i_s, lf_s))
    return hs.swapaxes(0, 1).reshape(B, L, H, dv), state


def mlstm_bidirectional(q, k, v, g, q_c, k_c, v_c, g_c):
    B = q.shape[0]
    init = (jnp.zeros((B, ML_HEADS, ML_QK_DIM, ML_V_DIM), jnp.float32),
            jnp.zeros((B, ML_HEADS, ML_QK_DIM), jnp.float32),
            jnp.zeros((B, ML_HEADS), jnp.float32))

    def fl(a):
        return jnp.flip(a, axis=1)

    hc_f, st_f = mlstm_chunkwise(q_c, k_c, v_c, g_c[:, :, 0, 0], g_c[:, :, 0, 1], init)
    hc_b, st_b = mlstm_chunkwise(fl(q_c), fl(k_c), fl(v_c), fl(g_c[:, :, 1, 0]), fl(g_c[:, :, 1, 1]), init)
    h_f, _ = mlstm_chunkwise(q, k, v, g[:, :, 0, 0], g[:, :, 0, 1], st_f)
    h_b, _ = mlstm_chunkwise(fl(q), fl(k), fl(v), fl(g[:, :, 1, 0]), fl(g[:, :, 1, 1]), st_b)
    return h_f + fl(h_b), hc_f + fl(hc_b)


def mlstm_readout(h_tilde, o_pre, norm_w):
    B, L = h_tilde.shape[:2]
    hn = rms_norm(h_tilde, norm_w.reshape(ML_HEADS, ML_V_DIM)).reshape(B, L, ML_V_W)
    return hn.astype(o_pre.dtype) * jax.nn.sigmoid(o_pre)


def merge_branches(att, ml, ga_pre, gm_pre, w_branch_att, w_branch_ml, w_out):
    y = jax.nn.sigmoid(ga_pre) * (att @ w_branch_att) + jax.nn.sigmoid(gm_pre) * (ml @ w_branch_ml)
    return y @ w_out


def conv_ffn(h, w_up, conv_w, conv_b, w_down):
    L = h.shape[1]
    u = h @ w_up
    r = CONV_W // 2
    up = jnp.pad(u, ((0, 0), (r, r), (0, 0)))
    acc = conv_b
    for j in range(CONV_W):
        acc = acc + up[:, j:j + L] * conv_w[j]
    a, g = jnp.split(acc, 2, axis=-1)
    return (jax.nn.silu(g) * a) @ w_down


def setup_inputs(seed: int = 0) -> dict:
    key = jax.random.key(seed)
    ks = jax.random.split(key, 24)
    f32 = jnp.float32

    def nrm(k, shape, scale):
        return jax.random.normal(k, shape, f32) * scale

    gate_offset = jnp.array([0.0, 3.0], f32).reshape(1, 1, 2, 1)
    return {
        "x": nrm(ks[0], (BATCH, SEQ, D_MODEL), 1.0),
        "c": nrm(ks[1], (BATCH, D_MODEL), 1.0),
        "ctx": nrm(ks[2], (BATCH, CTX_LEN, D_MODEL), 1.0),
        "c_ctx": nrm(ks[3], (D_MODEL,), 1.0),
        "w_mod": nrm(ks[4], (DEPTH, D_MODEL, 6 * D_MODEL), 0.5 * D_MODEL ** -0.5),
        "b_mod": nrm(ks[5], (DEPTH, 6 * D_MODEL), 0.02),
        "norm1_w": 1.0 + nrm(ks[6], (DEPTH, D_MODEL), 0.02),
        "w_in": nrm(ks[7], (DEPTH, D_MODEL, IN_W), D_MODEL ** -0.5),
        "q_norm_w": 1.0 + nrm(ks[8], (DEPTH, ATT_HEAD_DIM), 0.02),
        "k_norm_w": 1.0 + nrm(ks[9], (DEPTH, ATT_HEAD_DIM), 0.02),
        "attn_sink": nrm(ks[10], (DEPTH, ATT_HEADS), 0.5),
        "ml_gate_b": gate_offset + nrm(ks[11], (DEPTH, 2, 2, ML_HEADS), 0.3),
        "ml_norm_w": 1.0 + nrm(ks[12], (DEPTH, ML_V_W), 0.02),
        "w_branch_att": nrm(ks[13], (DEPTH, ATT_Q_W, D_MODEL), ATT_Q_W ** -0.5),
        "w_branch_ml": nrm(ks[14], (DEPTH, ML_V_W, D_MODEL), ML_V_W ** -0.5),
        "w_out": nrm(ks[15], (DEPTH, D_MODEL, D_MODEL), D_MODEL ** -0.5),
        "norm2_w": 1.0 + nrm(ks[16], (DEPTH, D_MODEL), 0.02),
        "w_up": nrm(ks[17], (DEPTH, D_MODEL, 2 * D_FF), D_MODEL ** -0.5),
        "conv_w": nrm(ks[18], (DEPTH, CONV_W, 2 * D_FF), CONV_W ** -0.5),
        "conv_b": nrm(ks[19], (DEPTH, 2 * D_FF), 0.02),
        "w_down": nrm(ks[20], (DEPTH, D_FF, D_MODEL), D_FF ** -0.5),
    }


def reference(x, c, ctx, c_ctx, w_mod, b_mod, norm1_w, w_in, q_norm_w, k_norm_w, attn_sink,
              ml_gate_b, ml_norm_w, w_branch_att, w_branch_ml, w_out, norm2_w, w_up, conv_w, conv_b, w_down):
    B, L, _ = x.shape
    C = ctx.shape[1]
    rows = L // GRID_W
    cos, sin = axial_rope(rows)
    for l in range(DEPTH):
        mod = jax.nn.silu(c) @ w_mod[l] + b_mod[l]
        mod_c = jax.nn.silu(c_ctx)[None, :] @ w_mod[l] + b_mod[l]
        sh1, sc1, g1, sh2, sc2, g2 = jnp.split(mod, 6, axis=-1)
        sh1c, sc1c, g1c, sh2c, sc2c, g2c = jnp.split(mod_c, 6, axis=-1)

        h = modulate(rms_norm(x, norm1_w[l]), sh1, sc1)
        hc = modulate(rms_norm(ctx, norm1_w[l]), sh1c, sc1c)
        aq, ak, av, mq, mk, mv, mo, mg, ga, gm = split_in_proj(h @ w_in[l])
        aqc, akc, avc, mqc, mkc, mvc, moc, mgc, gac, gmc = split_in_proj(hc @ w_in[l])

        q = apply_rope(rms_norm(heads(aq, ATT_HEADS, ATT_HEAD_DIM), q_norm_w[l]), cos, sin)
        k = apply_rope(rms_norm(heads(ak, ATT_KV_HEADS, ATT_HEAD_DIM), k_norm_w[l]), cos, sin)
        v = heads(av, ATT_KV_HEADS, ATT_HEAD_DIM)
        k_ctx = rms_norm(heads(akc, ATT_KV_HEADS, ATT_HEAD_DIM), k_norm_w[l])
        v_ctx = heads(avc, ATT_KV_HEADS, ATT_HEAD_DIM)
        att = windowed_gqa(q, k, v, k_ctx, v_ctx, attn_sink[l])

        gates = mg.reshape(B, L, 2, 2, ML_HEADS) + ml_gate_b[l]
        gates_c = mgc.reshape(B, C, 2, 2, ML_HEADS) + ml_gate_b[l]
        ht, ht_c = mlstm_bidirectional(
            heads(mq, ML_HEADS, ML_QK_DIM), heads(mk, ML_HEADS, ML_QK_DIM), heads(mv, ML_HEADS, ML_V_DIM), gates,
            heads(mqc, ML_HEADS, ML_QK_DIM), heads(mkc, ML_HEADS, ML_QK_DIM), heads(mvc, ML_HEADS, ML_V_DIM), gates_c)
        ml = mlstm_readout(ht, mo, ml_norm_w[l])

        y = merge_branches(att, ml, ga, gm, w_branch_att[l], w_branch_ml[l], w_out[l])
        x_mid = x + g1[:, None, :] * y

        if l < DEPTH - 1:
            q_c = rms_norm(heads(aqc, ATT_HEADS, ATT_HEAD_DIM), q_norm_w[l])
            att_c = context_gqa(q_c, k_ctx, v_ctx, attn_sink[l])
            ml_c = mlstm_readout(ht_c, moc, ml_norm_w[l])
            y_c = merge_branches(att_c, ml_c, gac, gmc, w_branch_att[l], w_branch_ml[l], w_out[l])
            ctx = ctx + g1c[:, None, :] * y_c
            ctx = ctx + g2c[:, None, :] * conv_ffn(modulate(rms_norm(ctx, norm2_w[l]), sh2c, sc2c),
                                                   w_up[l], conv_w[l], conv_b[l], w_down[l])

        x = x_mid + g2[:, None, :] * conv_ffn(modulate(rms_norm(x_mid, norm2_w[l]), sh2, sc2),
                                              w_up[l], conv_w[l], conv_b[l], w_down[l])
    return x
```

```python
import numpy as np
import concourse.bass as bass
import concourse.mybir as mybir
from concourse.bass_utils import run_bass_kernel_spmd

F32 = mybir.dt.float32
BF16 = mybir.dt.bfloat16
U8 = mybir.dt.uint8
AF = mybir.ActivationFunctionType
ALU = mybir.AluOpType
AX = mybir.AxisListType

D = 1024
L = 2048
C = 256
NT = 18
TOK = NT * 128
IN_W = 6672
DFF = 2816
EPS = 1e-6
O_AQ, O_AK, O_AV, O_MQ, O_MK, O_MV, O_MO, O_MG, O_GA, O_GM = 0, 1024, 1280, 1536, 2048, 2560, 3584, 4608, 4624, 5648

C_IDF, C_TRF, C_TRB, C_ONF, C_PMF, C_SWF, C_COS, C_SIN, NCF = 0, 128, 256, 384, 512, 640, 768, 2816, 4864
B_IDB, B_MLE, B_MGE, B_ONB, B_BD, B_PM, NCB = 0, 128, 640, 1152, 1280, 1408, 1536
B_NLE, B_NGE, NCB2 = 0, 512, 1024
V_N1W, V_N2W, V_BMOD, V_CC, V_CCX, V_MLNW, V_CW, V_CB, V_QNW, V_KNW, V_GB, V_SINK, NV = \
    0, 8, 16, 64, 72, 80, 88, 220, 264, 265, 266, 282, 298

SBUF_BASE = 16512
SBUF_LIMIT = 229312
SAME_ENGINE_SYNC = True
FMB = 6
REORDER = True
LOOKAHEAD = 0.15
FFN_SPLIT = True
PE_MARGIN = 0.0
NORM_SLACK = 4.0
MASK_ENG = "dve"
COS_ENG = "pool"


def _free_elems(ap):
    n = 1
    for v in ap.shape[1:]:
        n *= int(v)
    return n


class Sched:
    ENGS = ("pe", "act", "dve", "pool", "sp")

    def __init__(self, nc, eng_sems, dma_sems):
        self.nc = nc
        self.sem = dict(eng_sems)
        self.dma_sems = dma_sems
        self.nodes = []
        self.lastw = {}
        self.readers = {}
        self.base = {}
        self.ninst = 0
        self.prog = {e: [] for e in self.ENGS}

    @staticmethod
    def _excl(r, w):
        pr = [x for x in r if x[0] == "ps"]
        if not pr:
            return r, w
        return [x for x in r if x[0] != "ps"], list(w) + [x for x in pr if x not in w]

    def _add(self, eng, kind, insts, r, w, dur):
        r, w = self._excl(r, w)
        deps = set()
        for x in r:
            if x in self.lastw:
                deps.add(self.lastw[x])
            b = self.base.get(x[0])
            if b:
                deps |= b
        for x in w:
            if x in self.lastw:
                deps.add(self.lastw[x])
            deps.update(self.readers.get(x, ()))
            b = self.base.get(x[0])
            if b:
                deps |= b
        idx = len(self.nodes)
        self.nodes.append(dict(eng=eng, kind=kind, insts=insts, deps=deps, dur=dur, ph=getattr(self, "ph", 0)))
        for x in r:
            self.readers.setdefault(x, []).append(idx)
        for x in w:
            self.lastw[x] = idx
            self.readers[x] = []
        self.ninst += len(insts)
        return idx

    def op(self, e, method, r=(), w=(), **kw):
        n = _free_elems(kw["out"]) if "out" in kw else _free_elems(kw["ap"])
        from_psum = any(str(getattr(kw.get(k), "space", "")).upper().find("PSUM") >= 0 or "PSum" in str(type(getattr(kw.get(k), "tensor", None)))
                        for k in ("in_", "in0", "in1"))
        rate = {"act": 0.70e3 if from_psum else 0.92e3, "dve": 0.72e3 if from_psum else 0.90e3, "pool": 0.36e3}[e]
        dur = max(64, n) / rate + 0.1
        if "accum_out" in kw:
            dur += 0.1
        return self._add(e, "op", [(method, kw)], r, w, dur)

    def mm(self, insts, r=(), w=()):
        dur = 0.05
        for m, kw in insts:
            if m == "transpose":
                n = max(64, int(kw["in_"].shape[0]))
                f = 1.0
            else:
                n = max(64, _free_elems(kw["rhs"]))
                f = 4.0 if kw["rhs"].dtype == F32 else 1.0
            dur += f * n / 2.2e3 + 0.02
        return self._add("pe", "mm", insts, r, w, dur)

    def dma(self, e, r=(), w=(), **kw):
        nbytes = _free_elems(kw["out"]) * int(kw["out"].shape[0]) * 4
        dur = 2.0 + nbytes / 180e3
        return self._add(e, "dma", [("dma_start", kw)], r, w, dur)

    def events_of(self, names):
        out = set()
        names = set(names)
        for k, v in self.lastw.items():
            if k[0] in names:
                out.add(v)
        for k, lst in self.readers.items():
            if k[0] in names:
                out.update(lst)
        for n in names:
            out |= self.base.get(n, set())
        return out

    def wait_all(self, e):
        pass

    def schedule_emit(self, reorder=True):
        import heapq
        nodes = self.nodes
        N = len(nodes)
        LAT_X, LAT_S = 0.15, 0.08
        succs = [[] for _ in range(N)]
        indeg = [0] * N
        for i, nd in enumerate(nodes):
            nd["deps"] = {d for d in nd["deps"] if d != i}
            indeg[i] = len(nd["deps"])
            for d in nd["deps"]:
                succs[d].append(i)
        fin = [0.0] * N
        start = [0.0] * N
        order = []
        if reorder:
            DELTA = LOOKAHEAD
            bl = [0.0] * N
            for i in range(N - 1, -1, -1):
                m = 0.0
                for j in succs[i]:
                    v = bl[j] + LAT_X
                    if v > m:
                        m = v
                bl[i] = nodes[i]["dur"] + m
            ready_t = [0.0] * N
            prim = {e: [] for e in self.ENGS}
            sec = {e: [] for e in self.ENGS}
            free = {e: 0.0 for e in self.ENGS}
            for i in range(N):
                if indeg[i] == 0:
                    heapq.heappush(prim[nodes[i]["eng"]], (0.0, i))
            done = 0
            while done < N:
                best = None
                for e in self.ENGS:
                    if sec[e]:
                        c = free[e]
                    elif prim[e]:
                        c = max(free[e], prim[e][0][0])
                    else:
                        continue
                    if best is None or c < best[0]:
                        best = (c, e)
                c, e = best
                p, s_ = prim[e], sec[e]
                hz = c + DELTA
                while p and p[0][0] <= hz:
                    rt, i = heapq.heappop(p)
                    heapq.heappush(s_, (-bl[i], i))
                _, i = heapq.heappop(s_)
                nd = nodes[i]
                st = max(free[e], ready_t[i])
                start[i] = st
                if nd["kind"] == "dma":
                    free[e] = st + 0.15
                    fin[i] = st + nd["dur"]
                else:
                    free[e] = st + nd["dur"]
                    fin[i] = free[e]
                order.append(i)
                done += 1
                for j in succs[i]:
                    lat = LAT_S if nodes[j]["eng"] == e and nd["kind"] != "dma" else LAT_X
                    if nodes[j]["eng"] == "pe" and e != "pe":
                        lat += PE_MARGIN
                    t = fin[i] + lat + nd.get("lag", 0.0)
                    if t > ready_t[j]:
                        ready_t[j] = t
                    indeg[j] -= 1
                    if indeg[j] == 0:
                        heapq.heappush(prim[nodes[j]["eng"]], (ready_t[j], j))
            self.est_us = max(fin) if N else 0.0
            self.sim_start, self.sim_fin = start, fin
            order.sort(key=lambda i: (start[i], i))
        else:
            order = list(range(N))
        cnt = {e: 0 for e in self.ENGS}
        waited = {e: {} for e in self.ENGS}
        dma_val = {e: [0] * len(v) for e, v in self.dma_sems.items()}
        dma_next = {e: 0 for e in self.dma_sems}
        ev = [None] * N
        prog = self.prog

        def semh(key):
            return self.dma_sems[key[1]][key[2]] if isinstance(key, tuple) else self.sem[key]

        def wait(e, key, val):
            if key == e and not SAME_ENGINE_SYNC:
                return
            if waited[e].get(key, 0) >= val:
                return
            waited[e][key] = val
            prog[e].append(lambda eng, sem=semh(key), val=val: eng.wait_ge(sem, val))

        for i in order:
            nd = nodes[i]
            e = nd["eng"]
            if nd["kind"] == "dma":
                slots = self.dma_sems[e]
                sl = dma_next[e]
                dma_next[e] = (sl + 1) % len(slots)
                key = ("d", e, sl)
                if dma_val[e][sl] > 0:
                    wait(e, key, dma_val[e][sl])
            need = {}
            for d in nd["deps"]:
                k, v = ev[d]
                if need.get(k, 0) < v:
                    need[k] = v
            for k, v in need.items():
                wait(e, k, v)
            if nd["kind"] == "dma":
                dma_val[e][sl] += 16
                kw = nd["insts"][0][1]
                prog[e].append(lambda eng, kw=kw, sem=slots[sl]: eng.dma_start(**kw).then_inc(sem, 16))
                ev[i] = (key, dma_val[e][sl])
            else:
                cnt[e] += 1
                sem = self.sem[e]
                n = len(nd["insts"])
                for j, (m, kw) in enumerate(nd["insts"]):
                    if j == n - 1:
                        prog[e].append(lambda eng, m=m, kw=kw, sem=sem: getattr(eng, m)(**kw).then_inc(sem, 1))
                    else:
                        prog[e].append(lambda eng, m=m, kw=kw: getattr(eng, m)(**kw))
                ev[i] = (e, cnt[e])
        for e in self.ENGS:
            for k in self.ENGS:
                if cnt[k] > 0:
                    wait(e, k, cnt[k])
            for de, vals in dma_val.items():
                for sl, v in enumerate(vals):
                    if v > 0:
                        wait(e, ("d", de, sl), v)


class Pool:
    def __init__(self, nc, S, name, lo, hi):
        self.nc, self.S, self.name, self.lo, self.hi = nc, S, name, lo, hi
        self.ptr = lo
        self.names = []
        self.pending = set()
        self.gen = 0

    def reset(self):
        self.pending = self.S.events_of(self.names) | self.pending
        self.names = []
        self.ptr = self.lo
        self.gen += 1

    def alloc(self, name, shape, dtype):
        esz = 4 if dtype == F32 else (2 if dtype == BF16 else 1)
        n = 1
        for s in shape[1:]:
            n *= s
        nbytes = (n * esz + 31) // 32 * 32
        assert self.ptr + nbytes <= self.hi, f"pool {self.name} overflow allocating {name}: {self.ptr + nbytes - self.hi} over"
        uname = f"{name}_{self.name}{self.gen}"
        t = self.nc.alloc_sbuf_tensor_at(uname, list(shape), dtype, offset=self.ptr)
        self.ptr += nbytes
        self.names.append(name)
        self.S.base[name] = set(self.pending)
        return t


def host_constants():
    cf = np.zeros((128, NCF), np.float32)
    cb = np.zeros((128, NCB), np.float32)
    idx = np.arange(128)
    cf[:, C_IDF:C_IDF + 128] = np.eye(128, dtype=np.float32)
    le = (idx[:, None] <= idx[None, :]).astype(np.float32)
    ge = (idx[:, None] >= idx[None, :]).astype(np.float32)
    cf[:, C_TRF:C_TRF + 128] = le
    cf[:, C_TRB:C_TRB + 128] = ge
    cf[:, C_ONF:C_ONF + 128] = 1.0
    partner = np.where((idx % 64) < 32, idx + 32, idx - 32)
    pm = np.zeros((128, 128), np.float32)
    pm[partner, idx] = 1.0
    cf[:, C_PMF:C_PMF + 128] = pm
    sw = np.zeros((128, 128), np.float32)
    sw[(idx + 64) % 128, idx] = 1.0
    cf[:, C_SWF:C_SWF + 128] = sw
    t = np.arange(L)
    row = (t // 64).astype(np.float32)
    col = (t % 64).astype(np.float32)
    inv_freq = (np.float32(10000.0) ** (-np.arange(16, dtype=np.float32) / np.float32(16))).astype(np.float32)
    ang = np.concatenate([row[:, None] * inv_freq[None, :], col[:, None] * inv_freq[None, :]], axis=1).astype(np.float32)
    cosv = np.cos(ang).astype(np.float32)
    sinv = np.sin(ang).astype(np.float32)
    j = idx % 32
    half = (idx % 64) // 32
    cf[:, C_COS:C_COS + L] = cosv[:, j].T
    sgn = np.where(half == 0, -1.0, 1.0).astype(np.float32)
    cf[:, C_SIN:C_SIN + L] = sinv[:, j].T * sgn[:, None]
    cb[:, B_IDB:B_IDB + 128] = np.eye(128, dtype=np.float32)
    cb[:, B_MLE:B_MLE + 512] = np.tile(le, (1, 4))
    cb[:, B_MGE:B_MGE + 512] = np.tile(ge, (1, 4))
    cb[:, B_ONB:B_ONB + 128] = 1.0
    bd = np.zeros((128, 128), np.float32)
    bd[:64, :64] = 1.0 / 64
    bd[64:, 64:] = 1.0 / 64
    cb[:, B_BD:B_BD + 128] = bd
    cb[:, B_PM:B_PM + 128] = pm
    cb2 = np.zeros((128, NCB2), np.float32)
    cb2[:, B_NLE:B_NLE + 512] = (np.tile(le, (1, 4)) - 1.0) * 30000.0
    cb2[:, B_NGE:B_NGE + 512] = (np.tile(ge, (1, 4)) - 1.0) * 30000.0
    return cf, cb, cb2


def host_vecs(b, c, c_ctx, b_mod, norm1_w, norm2_w, ml_norm_w, conv_w, conv_b, q_norm_w, k_norm_w, ml_gate_b, attn_sink):
    v = np.zeros((128, NV), np.float32)

    def pc(vec):
        return np.ascontiguousarray(vec.reshape(-1, 128).T)

    v[:, V_N1W:V_N1W + 8] = pc(norm1_w[0])
    v[:, V_N2W:V_N2W + 8] = pc(norm2_w[0])
    v[:, V_BMOD:V_BMOD + 48] = pc(b_mod[0])
    v[:, V_CC:V_CC + 8] = pc(c[b])
    v[:, V_CCX:V_CCX + 8] = pc(c_ctx)
    v[:, V_MLNW:V_MLNW + 8] = pc(ml_norm_w[0])
    for j in range(3):
        v[:, V_CW + j * 44:V_CW + (j + 1) * 44] = pc(conv_w[0, j])
    v[:, V_CB:V_CB + 44] = pc(conv_b[0])
    v[:, V_QNW] = np.tile(q_norm_w[0], 2)
    v[:, V_KNW] = np.tile(k_norm_w[0], 2)
    v[:, V_GB:V_GB + 16] = ml_gate_b[0].reshape(1, 16)
    v[:, V_SINK:V_SINK + 16] = attn_sink[0].reshape(1, 16)
    return v


def build_program(debug=False, stop_after=99):
    nc = bass.Bass("TRN2", target_bir_lowering=False)

    def din(name, shape):
        return nc.dram_tensor(name, list(shape), F32, kind="ExternalInput").ap()

    x_d = din("x", [L, D])
    ctx_d = din("ctx", [C, D])
    vecs_d = din("vecs", [128, NV])
    bmod_d = din("bmod_row", [1, 6144])
    cf_d = din("cstf", [128, NCF])
    cb_d = din("cstb", [128, NCB])
    cb2_d = din("cstb2", [128, NCB2])
    wmod_d = din("w_mod", [D, 6144])
    win_d = din("w_in", [D, IN_W])
    wba_d = din("w_ba", [D, D])
    wbm_d = din("w_bm", [D, D])
    wout_d = din("w_out", [D, D])
    wup_d = din("w_up", [D, 2 * DFF])
    wdn_d = din("w_down", [DFF, D])
    out_d = nc.dram_tensor("out", [L, D], F32, kind="ExternalOutput").ap()
    xmid_d = nc.dram_tensor("xmid_scr", [L, D], F32, kind="Internal").ap()
    dbg = {}

    def dbg_out(name, shape):
        if not debug:
            return None
        t = nc.dram_tensor("dbg_" + name, list(shape), F32, kind="ExternalOutput").ap()
        dbg[name] = t
        return t

    arena = nc.alloc_sbuf_tensor("arena", [128, SBUF_LIMIT - SBUF_BASE], U8)
    psb = [nc.alloc_psum_tensor(f"psb{i}", [128, 512], F32) for i in range(8)]

    def PS(i):
        return ("ps", i)

    import contextlib
    with contextlib.ExitStack() as es:
        eng_sems = {e: es.enter_context(nc.semaphore("s_" + e)) for e in Sched.ENGS}
        dma_sems = {"sp": [es.enter_context(nc.semaphore(f"dsp{i}")) for i in range(12)],
                    "pool": [es.enter_context(nc.semaphore(f"dpl{i}")) for i in range(24)]}
        S = Sched(nc, eng_sems, dma_sems)
        emit(nc, S, locals())
        S.schedule_emit(reorder=REORDER)
        block = es.enter_context(nc.Block())

        @block.tensor
        def _(eng):
            for f in S.prog["pe"]:
                f(eng)

        @block.scalar
        def _(eng):
            for f in S.prog["act"]:
                f(eng)

        @block.vector
        def _(eng):
            for f in S.prog["dve"]:
                f(eng)

        @block.gpsimd
        def _(eng):
            for f in S.prog["pool"]:
                f(eng)

        @block.sync
        def _(eng):
            for f in S.prog["sp"]:
                f(eng)
    return nc, list(dbg.keys()), S


def wview(w_d, c0, n):
    return w_d.rearrange("(kc p) n -> p kc n", p=128)[:, :, c0:c0 + n]


def emit(nc, S, env):
    x_d, ctx_d, vecs_d, bmod_d, cf_d, cb_d = env["x_d"], env["ctx_d"], env["vecs_d"], env["bmod_d"], env["cf_d"], env["cb_d"]
    wmod_d, win_d, wba_d, wbm_d, wout_d, wup_d, wdn_d = (env[k] for k in ("wmod_d", "win_d", "wba_d", "wbm_d", "wout_d", "wup_d", "wdn_d"))
    out_d, xmid_d, psb, dbg_out, debug = env["out_d"], env["xmid_d"], env["psb"], env["dbg_out"], env["debug"]
    stop_after = env.get("stop_after", 99)
    PSK = lambda i: ("ps", i)
    MUL, ADD, SUB, MAX = ALU.mult, ALU.add, ALU.subtract, ALU.max

    B0 = SBUF_BASE
    o1 = B0 + 26880
    o2 = o1 + 36864
    o3 = o2 + 65536
    PERS = Pool(nc, S, "pers", B0, o1)
    PA = Pool(nc, S, "pa", o1, o2)
    PC = Pool(nc, S, "pc", o2, o3)
    PW = Pool(nc, S, "pw", o3, SBUF_LIMIT)

    def wload(dst, w_d, c0, n, key, kcs=8, row0=0):
        src = w_d[row0:row0 + kcs * 128, :].rearrange("(kc p) n -> p kc n", p=128)[:, :, c0:c0 + n]
        return S.dma("pool", w=[key], out=dst, in_=src, max_dma_last_dim=8192)

    def dump(name, ap, shape, r):
        if not debug:
            return
        t = dbg_out(name, shape)
        S.dma("pool", r=r, w=[("dbg", name)], out=t, in_=ap, max_dma_last_dim=2048)

    def finish():
        for e in ("sp", "pe", "act", "dve", "pool"):
            S.wait_all(e)

    cstf = PERS.alloc("cstf", [128, C_SWF], F32)
    cstb = PERS.alloc("cstb", [128, NCB], BF16)
    vecs = PERS.alloc("vecs", [128, NV], F32)
    modT = PERS.alloc("modT", [128, 48, 2], F32)
    aff = PERS.alloc("aff", [128, 6, 8], F32)
    wq8 = PERS.alloc("wq8", [128, 2], F32)
    g1b = PERS.alloc("g1b", [128, 1024], F32)
    g2b = PERS.alloc("g2b", [128, 1024], F32)
    wkp = PERS.alloc("wkp", [128, 2, NT, 4], F32)
    eden = PERS.alloc("eden", [128, 2, NT, 4], F32)
    rbt = PERS.alloc("rbt", [128, 2, NT, 4], F32)
    vatt = PERS.alloc("vatt", [128, NT, 256], BF16)
    sexp = PERS.alloc("sexp", [128, 16], F32)
    S.dma("sp", w=[("cstf",)], out=cstf[:], in_=cf_d[:, 0:C_SWF])
    S.dma("sp", w=[("vecs",)], out=vecs[:], in_=vecs_d)
    S.dma("pool", w=[("cstb",)], out=cstb[:], in_=cb_d)
    identf = cstf[:, C_IDF:C_IDF + 128]
    trif = cstf[:, C_TRF:C_TRF + 128]
    trib = cstf[:, C_TRB:C_TRB + 128]
    onesf = cstf[:, C_ONF:C_ONF + 128]
    permf = cstf[:, C_PMF:C_PMF + 128]
    identb = cstb[:, B_IDB:B_IDB + 128]
    bdiag = cstb[:, B_BD:B_BD + 128]
    permb = cstb[:, B_PM:B_PM + 128]
    CF, CB, VK, GK = ("cstf",), ("cstb",), ("vecs",), ("gates",)

    S.ph = 1
    scb = PC.alloc("scb", [128, 8, 2], BF16)
    screp = PC.alloc("screp", [128, 8, 128], BF16)
    bmrow = PC.alloc("bmrow", [1, 4, 512], F32)
    wmod = [PC.alloc(f"wmod{i}", [128, 8, 512], BF16) for i in range(2)]
    for i_, blk_ in enumerate((4, 5, 10, 11)):
        S.dma("sp", w=[("bmrow", i_)], out=bmrow[0:1, i_, :], in_=bmod_d[0:1, blk_ * 512:(blk_ + 1) * 512])
    S.op("act", "activation", r=[VK], w=[("scb",)], out=scb[:, :, 0], in_=vecs[:, V_CC:V_CC + 8], func=AF.Silu)
    S.op("act", "activation", r=[VK], w=[("scb",)], out=scb[:, :, 1], in_=vecs[:, V_CCX:V_CCX + 8], func=AF.Silu)
    S.op("dve", "tensor_copy", r=[("scb",)], w=[("screp",)], out=screp[:],
         in_=scb[:, :, 0:1].to_broadcast([128, 8, 128]))
    psm = psb[6][:, 0:96].rearrange("p (a b) -> p a b", b=2)
    def mk_aff(ia, ib, sc_ch, sh_ch, which, nw_col):
        mk = [("modT", sc_ch // 16), ("modT", sh_ch // 16)]
        S.op("dve", "scalar_tensor_tensor", r=mk + [VK], w=[("aff", ia)], out=aff[:, ia, :],
             in0=modT[:, sc_ch:sc_ch + 8, which], scalar=1.0, in1=vecs[:, nw_col:nw_col + 8], op0=ADD, op1=MUL)
        S.op("dve", "tensor_copy", r=mk, w=[("aff", ib)], out=aff[:, ib, :], in_=modT[:, sh_ch:sh_ch + 8, which])
    def wmod_block(blk, gate=()):
            wt = wmod[blk % 2]
            wk = (f"wmod{blk % 2}",)
            S.dma("pool", r=list(gate), w=[wk], out=wt[:], in_=wmod_d.rearrange("(kc p) n -> p kc n", p=128)[:, :, blk * 512:(blk + 1) * 512], max_dma_last_dim=8192)
            for j in range(4):
                ch = blk * 4 + j
                S.mm([("matmul", dict(out=psm[:, ch, :], lhsT=wt[:, kc, j * 128:(j + 1) * 128], rhs=scb[:, kc, :],
                                      start=(kc == 0), stop=(kc == 7))) for kc in range(8)],
                     r=[wk, ("scb",)], w=[PSK(6)])
            if blk % 4 == 3:
                c0_ = (blk // 4) * 16
                S.op("dve", "tensor_tensor", r=[PSK(6), VK], w=[("modT", blk // 4)], out=modT[:, c0_:c0_ + 16, :], in0=psm[:, c0_:c0_ + 16, :],
                     in1=vecs[:, V_BMOD + c0_:V_BMOD + c0_ + 16][:, :, None].to_broadcast([128, 16, 2]), op=ADD)
            if blk == 3:
                mk_aff(0, 1, 8, 0, 0, V_N1W)
                mk_aff(2, 3, 8, 0, 1, V_N1W)
            if blk == 11:
                mk_aff(4, 5, 32, 24, 0, V_N2W)
            if blk in (4, 5, 10, 11):
                bank = 7
                ins = [("matmul", dict(out=psb[bank][:], lhsT=screp[:, kc, :], rhs=wt[:, kc, :],
                                       start=(kc == 0), stop=False)) for kc in range(8)]
                bi_ = (4, 5, 10, 11).index(blk)
                ins.append(("matmul", dict(out=psb[bank][:], lhsT=onesf[0:1, :], rhs=bmrow[0:1, bi_, :],
                                           start=False, stop=True)))
                S.mm(ins, r=[wk, ("screp",), ("bmrow", bi_), CF], w=[PSK(bank)])
                dst = (g1b if blk < 6 else g2b)[:, (blk % 2) * 512:(blk % 2 + 1) * 512]
                S.op("dve", "tensor_copy", r=[PSK(bank)], w=[("g1b",) if blk < 6 else ("g2b",)], out=dst, in_=psb[bank][:])


    for blk in range(4):
        wmod_block(blk)
    S.op("act", "mul", r=[VK], w=[("wq8",)], out=wq8[:, 0:1], in_=vecs[:, V_QNW:V_QNW + 1], mul=0.125)
    S.op("act", "copy", r=[VK], w=[("wq8",)], out=wq8[:, 1:2], in_=vecs[:, V_KNW:V_KNW + 1])
    S.op("act", "activation", r=[VK], w=[("sexp",)], out=sexp[:], in_=vecs[:, V_SINK:V_SINK + 16], func=AF.Exp)

    S.ph = 2
    hT = PA.alloc("hT", [128, 8, TOK], BF16)

    def alloc_norm_bufs(P, nbuf):
        xin = [P.alloc(f"xin{i}", [128, 1024], F32) for i in range(nbuf)]
        xn = [P.alloc(f"xn{i}", [128, 1024], F32) for i in range(nbuf)]
        junk = P.alloc("junk", [128, 1024], BF16)
        st4 = P.alloc("st4", [128, nbuf, 4], F32)
        return xin, xn, junk, st4

    def norm_tile(junk, src_key, src_ap, xn_t, xn_key, stv, st_key, ndim):
        S.op("act", "activation", r=[src_key], w=[("junk",), st_key], out=junk[:, 0:ndim], in_=src_ap, func=AF.Square,
             accum_out=stv[:, 0:1])
        S.op("act", "activation", r=[st_key], w=[st_key], out=stv[:, 1:2], in_=stv[:, 0:1], func=AF.Ln,
             scale=1.0 / ndim, bias=EPS)
        S.op("act", "activation", r=[st_key], w=[st_key], out=stv[:, 2:3], in_=stv[:, 1:2], func=AF.Exp, scale=-0.5)
        S.op("dve", "tensor_scalar", r=[src_key, st_key], w=[xn_key], out=xn_t, in0=src_ap, scalar1=stv[:, 2:3],
             scalar2=None, op0=MUL)

    def to_featmajor(xn_t, xn_key, dstT, dst_key, col0, ia, ib, banks):
        for half in range(2):
            bk = banks[half]
            S.mm([("transpose", dict(out=psb[bk][:, j * 128:(j + 1) * 128], in_=xn_t[:, (half * 4 + j) * 128:(half * 4 + j + 1) * 128],
                                     identity=identf)) for j in range(4)], r=[xn_key, CF], w=[PSK(bk)])
            for j in range(4):
                kc = half * 4 + j
                S.op("act", "activation", r=[PSK(bk), ("aff", ia), ("aff", ib)], w=[dst_key], out=dstT[:, kc, col0:col0 + 128],
                     in_=psb[bk][:, j * 128:(j + 1) * 128], func=AF.Identity, scale=aff[:, ia, kc:kc + 1], bias=aff[:, ib, kc:kc + 1])

    xin, xn, junk, st4 = alloc_norm_bufs(PC, 3)
    for t in range(NT):
        bi = t % 3
        src = ctx_d[t * 128:(t + 1) * 128, :] if t < 2 else x_d[(t - 2) * 128:(t - 1) * 128, :]
        S.dma("sp", w=[(f"xin{bi}",)], out=xin[bi][:], in_=src)
        norm_tile(junk, (f"xin{bi}",), xin[bi][:], xn[bi][:], (f"xn{bi}",), st4[:, bi, :], ("st4", bi), 1024)
        ia, ib = (2, 3) if t < 2 else (0, 1)
        to_featmajor(xn[bi], (f"xn{bi}",), hT, ("hT", t), t * 128, ia, ib, (2 * bi, 2 * bi + 1))
    for blk in range(4, 12):
        wmod_block(blk, gate=[("hT", min(NT - 1, 2 * (blk - 3) + 1))])
    dump("modT", modT[:], [128, 48, 2], [("modT", i) for i in range(3)])
    dump("g1b", g1b[:], [128, 1024], [("g1b",)])
    dump("hT", hT[:], [128, 8, TOK], [("hT", t) for t in range(NT)])
    if stop_after <= 2:
        return finish()

    S.ph = 3
    w3g = PC.alloc("w3g", [128, 8, 272], BF16)
    gpre = PC.alloc("gpre", [128, NT, 16], F32)
    wload(w3g[:, :, 0:16], win_d, O_MG, 16, ("w3g",))
    wload(w3g[:, :, 16:272], win_d, O_AV, 256, ("w3g",))
    for t in range(NT):
        bk = t % 2
        S.mm([("matmul", dict(out=psb[bk][:, 0:272], lhsT=hT[:, kc, t * 128:(t + 1) * 128], rhs=w3g[:, kc, :],
                              start=(kc == 0), stop=(kc == 7))) for kc in range(8)], r=[("hT", t), ("w3g",)], w=[PSK(bk)])
        S.op("act", "copy", r=[PSK(bk)], w=[("gpre",)], out=gpre[:, t, :], in_=psb[bk][:, 0:16])
        S.op("dve", "tensor_copy", r=[PSK(bk)], w=[("vatt",)], out=vatt[:, t, :], in_=psb[bk][:, 16:272])

    S.ph = 4
    GI = PC.alloc("GI", [128, 2, NT, 4], F32)
    GF = PC.alloc("GF", [128, 2, NT, 4], F32)
    SPt = PC.alloc("SPt", [128, 2, NT, 4], F32)
    BN = PC.alloc("BN", [128, 2, NT, 4], F32)
    Gt = PC.alloc("Gt", [128, 2, NT, 4], F32)
    D1 = PC.alloc("D1", [128, 2, NT, 4], F32)
    D2 = PC.alloc("D2", [128, 2, NT, 4], F32)
    gmx = PC.alloc("gmx", [128, 2], F32)
    ROW = PC.alloc("ROW", [1, 288], F32)
    GR = PC.alloc("GR", [1, 288], F32)
    Mst = PC.alloc("Mst", [1, 2, 4], F32)
    gp5 = gpre[:].rearrange("p c (d g h) -> p c d g h", d=2, g=2)
    gbv = vecs[:, V_GB:V_GB + 16].rearrange("p (d g h) -> p d g h", d=2, g=2)
    for d in range(2):
        S.op("dve", "tensor_tensor", r=[("gpre",), VK], w=[("GI",)], out=GI[:, d], in0=gp5[:, :, d, 0, :],
             in1=gbv[:, d, 0, :][:, None, :].to_broadcast([128, NT, 4]), op=ADD)
        S.op("dve", "tensor_tensor", r=[("gpre",), VK], w=[("GF",)], out=GF[:, d], in0=gp5[:, :, d, 1, :],
             in1=gbv[:, d, 1, :][:, None, :].to_broadcast([128, NT, 4]), op=ADD)
    fl = lambda t: t[:].rearrange("p d c h -> p (d c h)")
    S.op("act", "activation", r=[("GF",)], w=[("SPt",)], out=fl(SPt), in_=fl(GF), func=AF.Exp, scale=-1.0)
    S.op("act", "activation", r=[("SPt",)], w=[("SPt",)], out=fl(SPt), in_=fl(SPt), func=AF.Ln, bias=1.0)
    S.mm([("matmul", dict(out=psb[0][:, 0:72], lhsT=trif, rhs=fl(SPt)[:, 0:72], start=True, stop=True)),
          ("matmul", dict(out=psb[0][:, 72:144], lhsT=trib, rhs=fl(SPt)[:, 72:144], start=True, stop=True)),
          ("matmul", dict(out=psb[1][:, 0:144], lhsT=onesf, rhs=fl(SPt), start=True, stop=True))],
         r=[("SPt",), CF], w=[PSK(0), PSK(1)])
    S.op("dve", "tensor_copy", r=[PSK(0)], w=[("BN",)], out=fl(BN), in_=psb[0][:, 0:144])
    S.op("dve", "tensor_tensor", r=[("BN",), ("GI",)], w=[("Gt",)], out=fl(Gt), in0=fl(GI), in1=fl(BN), op=ADD)
    S.op("dve", "tensor_copy", r=[PSK(1)], w=[("ROW",)], out=ROW[0:1, 144:288], in_=psb[1][0:1, 0:144])
    S.mm([("transpose", dict(out=psb[2][0:72, 0:128], in_=fl(Gt)[:, 0:72], identity=identf)),
          ("transpose", dict(out=psb[2][0:72, 128:256], in_=fl(Gt)[:, 72:144], identity=identf))],
         r=[("Gt",), CF], w=[PSK(2)])
    for d in range(2):
        S.op("dve", "tensor_reduce", r=[PSK(2)], w=[("gmx",)], out=gmx[0:72, d:d + 1], in_=psb[2][0:72, d * 128:(d + 1) * 128],
             axis=AX.X, op=MAX)
    S.mm([("transpose", dict(out=psb[3][0:1, 0:72], in_=gmx[0:72, 0:1], identity=identf[0:72, 0:72])),
          ("transpose", dict(out=psb[3][0:1, 72:144], in_=gmx[0:72, 1:2], identity=identf[0:72, 0:72]))],
         r=[("gmx",), CF], w=[PSK(3)])
    S.op("dve", "tensor_copy", r=[PSK(3)], w=[("ROW",)], out=ROW[0:1, 0:144], in_=psb[3][0:1, 0:144])
    S.op("dve", "memset", w=[("Mst", 0), ("Mst", 1)], ap=Mst[0:1], constant=0.0)
    order = [list(range(NT)), [1, 0] + list(range(NT - 1, 1, -1))]
    for k in range(NT):
        for d in range(2):
            c = order[d][k]
            i0 = d * 72 + c * 4
            gsl = GR[0:1, i0:i0 + 4]
            S.op("dve", "tensor_tensor", r=[("Mst", d), ("ROW",)], w=[("GR",)], out=gsl, in0=Mst[0:1, d, :], in1=ROW[0:1, i0:i0 + 4], op=MAX)
            S.op("dve", "tensor_tensor", r=[("Mst", d), ("GR",)], w=[("GR",)], out=GR[0:1, 144 + i0:144 + i0 + 4], in0=Mst[0:1, d, :], in1=gsl, op=SUB)
            S.op("dve", "tensor_tensor", r=[("GR",), ("ROW",)], w=[("Mst", d)], out=Mst[0:1, d, :], in0=gsl, in1=ROW[0:1, 144 + i0:144 + i0 + 4], op=SUB)
    S.mm([("matmul", dict(out=psb[4][:, 0:288], lhsT=onesf[0:1, :], rhs=GR[0:1, :], start=True, stop=True))],
         r=[("GR",), CF], w=[PSK(4)])
    S.op("dve", "tensor_tensor", r=[("Gt",), PSK(4)], w=[("D1",)], out=fl(D1), in0=fl(Gt), in1=psb[4][:, 0:144], op=SUB)
    S.op("act", "activation", r=[("D1",)], w=[GK], out=fl(wkp), in_=fl(D1), func=AF.Exp, bias=float(-0.5 * np.log(128.0)))
    S.op("dve", "tensor_tensor", r=[("BN",), PSK(4)], w=[("D2",)], out=fl(D2), in0=fl(BN), in1=psb[4][:, 0:144], op=SUB)
    S.op("act", "activation", r=[("D2",)], w=[GK], out=fl(eden), in_=fl(D2), func=AF.Exp)
    S.op("act", "activation", r=[PSK(4)], w=[GK], out=fl(rbt), in_=psb[4][:, 144:288], func=AF.Exp)
    dump("wkp", wkp[:], [128, 2, NT, 4], [GK])
    dump("eden", eden[:], [128, 2, NT, 4], [GK])
    dump("rbt", rbt[:], [128, 2, NT, 4], [GK])
    dump("vatt", vatt[:], [128, NT, 256], [("vatt",)])
    if stop_after <= 4:
        return finish()

    S.ph = 5
    PC.reset()
    PW.reset()
    mlT = PC.alloc("mlT", [128, 8, L], BF16)
    attT = PC.alloc("attT", [128, 8, L], BF16)
    wq_hs = [PW.alloc(f"wq_h{i}", [128, 8, 128], BF16) for i in range(2)]
    wk_hs = [PW.alloc(f"wk_h{i}", [128, 8, 128], BF16) for i in range(2)]
    wv_h = PW.alloc("wv_h", [128, 8, 256], BF16)
    wo_h = PW.alloc("wo_h", [128, 8, 256], BF16)
    qTs = [PW.alloc(f"qT{i}", [128, L], BF16) for i in range(2)]
    kTs = [PW.alloc(f"kT{i}", [128, L], BF16) for i in range(2)]
    osig = PW.alloc("osig", [128, 2, L], BF16)
    ktoks = [PW.alloc(f"ktok{i}", [128, NT, 128], BF16) for i in range(2)]
    vexts = [PW.alloc(f"vext{i}", [128, NT, 260], BF16) for i in range(2)]
    hbuf = PW.alloc("hbuf", [128, 16, 256], BF16)
    C32 = PW.alloc("C32", [128, 2, 260], F32)
    Cs = PW.alloc("Cs", [128, 2, 260], BF16)
    Sm = PW.alloc("Sm", [128, 2, 2, 128], BF16)
    kw = PW.alloc("kw", [128, 2, 2, 128], BF16)
    hn = PW.alloc("hn", [128, 256], BF16)
    den = PW.alloc("den", [128, 2, 2], F32)
    fst = PW.alloc("fst", [128, 4], F32)
    junk5 = PW.alloc("junk5", [128, 256], BF16)
    maskap = [cstb[:, B_MLE:B_MLE + 128], cstb[:, B_MGE:B_MGE + 128]]
    for i_ in range(2):
        S.op("dve", "memset", w=[(f"vext{i_}", "ones")], ap=vexts[i_][:, :, 256:260], constant=0.0)
        S.op("dve", "memset", r=[], w=[(f"vext{i_}", "ones")], ap=vexts[i_][:, :, 256:257], constant=1.0)
    if stop_after <= 4.05:
        return finish()
    for hd in range(4):
        hb_ = hd % 2
        wq_h, wk_h, qT, kT, ktok = wq_hs[hb_], wk_hs[hb_], qTs[hb_], kTs[hb_], ktoks[hb_]
        QT, KT, KTOK, WQ, WK = f"qT{hb_}", f"kT{hb_}", f"ktok{hb_}", (f"wq_h{hb_}",), (f"wk_h{hb_}",)
        vext, VX = vexts[hb_], f"vext{hb_}"
        wload(wq_h[:], win_d, O_MQ + hd * 128, 128, WQ)
        wload(wk_h[:], win_d, O_MK + hd * 128, 128, WK)
        wload(wv_h[:], win_d, O_MV + hd * 256, 256, ("wv_h",))
        wload(wo_h[:], win_d, O_MO + hd * 256, 256, ("wo_h",))
        nb = 0
        for tg in range(4):
            tok0 = 256 + tg * 512
            rk = [("hT", 2 + tg * 4 + i) for i in range(4)]

            def fm(wt, wkey, cols):
                nonlocal nb
                bk = FMB + (nb % 2)
                nb += 1
                S.mm([("matmul", dict(out=psb[bk][:], lhsT=wt[:, kc, cols], rhs=hT[:, kc, tok0:tok0 + 512],
                                      start=(kc == 0), stop=(kc == 7))) for kc in range(8)], r=[wkey] + rk, w=[PSK(bk)])
                return bk
            bk = fm(wq_h, WQ, slice(0, 128))
            S.op("act", "copy", r=[PSK(bk)], w=[(QT, tg)], out=qT[:, tg * 512:(tg + 1) * 512], in_=psb[bk][:])
            if stop_after <= 4.1:
                return finish()
            bk = fm(wk_h, WK, slice(0, 128))
            S.op("dve", "tensor_copy", r=[PSK(bk)], w=[(KT, tg)], out=kT[:, tg * 512:(tg + 1) * 512], in_=psb[bk][:])
        if stop_after <= 4.15:
            return finish()
        for t in range(NT):
            bk = FMB + (t % 2)
            kpart = [("matmul", dict(out=psb[bk][:, 0:128], lhsT=hT[:, kc, t * 128:(t + 1) * 128], rhs=wk_h[:, kc, :],
                                     start=(kc == 0), stop=(kc == 7))) for kc in range(8)] if t < 2 else []
            S.mm(kpart +
                 [("matmul", dict(out=psb[bk][:, 128:384], lhsT=hT[:, kc, t * 128:(t + 1) * 128], rhs=wv_h[:, kc, :],
                                  start=(kc == 0), stop=(kc == 7))) for kc in range(8)],
                 r=[("hT", t), WK, ("wv_h",)], w=[PSK(bk)])
            if stop_after <= 4.16:
                return finish()
            if t < 2:
                S.op("dve", "tensor_copy", r=[PSK(bk)], w=[(KTOK, t)], out=ktok[:, t, :], in_=psb[bk][:, 0:128])
            if stop_after <= 4.17:
                return finish()
            S.op("act", "copy", r=[PSK(bk)], w=[(VX, t)], out=vext[:, t, 0:256], in_=psb[bk][:, 128:384])
            if stop_after <= 4.18:
                return finish()
        if stop_after <= 4.19:
            return finish()
        for tg in range(4):
            bk = FMB + (tg % 2)
            ptb = psb[bk][:].bitcast(BF16)
            S.mm([("transpose", dict(out=ptb[:, j * 128:(j + 1) * 128], in_=kT[:, (tg * 4 + j) * 128:(tg * 4 + j + 1) * 128], identity=identb))
                  for j in range(4)], r=[(KT, tg), CB], w=[PSK(bk)])
            S.op("dve", "tensor_copy", r=[PSK(bk)], w=[(KTOK, 2 + tg * 4 + j) for j in range(4)], out=ktok[:, 2 + tg * 4:2 + tg * 4 + 4, :],
                 in_=ptb[:, 0:512].rearrange("p (j d) -> p j d", j=4))
        for tg in range(4):
            tok0 = 256 + tg * 512
            rk = [("hT", 2 + tg * 4 + i) for i in range(4)]
            for e in range(2):
                bk = fm(wo_h, ("wo_h",), slice(e * 128, (e + 1) * 128))
                S.op("act", "activation", r=[PSK(bk)], w=[("osig", tg)], out=osig[:, e, tg * 512:(tg + 1) * 512], in_=psb[bk][:],
                     func=AF.Sigmoid)
        S.op("dve", "memset", w=[("C32", 0), ("C32", 1)], ap=C32[:], constant=0.0)
        if stop_after <= 4.2:
            return finish()

        def finalize(lc, d):
            pstb = psb[2 + d][:].bitcast(BF16)[:, 768:1024]
            S.op("act", "activation", r=[("hbuf", lc)], w=[("junk5",), ("fst",)], out=junk5[:], in_=hbuf[:, lc, :], func=AF.Square,
                 accum_out=fst[:, 0:1])
            S.op("act", "activation", r=[("fst",)], w=[("fst",)], out=fst[:, 1:2], in_=fst[:, 0:1], func=AF.Ln, scale=1.0 / 256, bias=EPS)
            S.op("act", "activation", r=[("fst",)], w=[("fst",)], out=fst[:, 2:3], in_=fst[:, 1:2], func=AF.Exp, scale=-0.5)
            S.op("pool", "tensor_scalar", r=[("hbuf", lc), ("fst",)], w=[("hn",)], out=hn[:], in0=hbuf[:, lc, :], scalar1=fst[:, 2:3],
                 scalar2=1.0, op0=MUL, op1=MUL)
            S.mm([("transpose", dict(out=pstb[:, e * 128:(e + 1) * 128], in_=hn[:, e * 128:(e + 1) * 128], identity=identb))
                  for e in range(2)], r=[("hn",), CB], w=[PSK(2 + d)])
            for e in range(2):
                ch = hd * 2 + e
                S.op("dve", "scalar_tensor_tensor", r=[PSK(2 + d), VK, ("osig", lc // 4)], w=[("mlT", ch, lc)],
                     out=mlT[:, ch, lc * 128:(lc + 1) * 128], in0=pstb[:, e * 128:(e + 1) * 128],
                     scalar=vecs[:, V_MLNW + ch:V_MLNW + ch + 1], in1=osig[:, e, lc * 128:(lc + 1) * 128], op0=MUL, op1=MUL)

        def stage_a(k, d):
            c = order[d][k]
            p = k % 2
            lc = c - 2
            wkc = wkp[:, d, c, hd:hd + 1]
            if c >= 2:
                tsl = slice(lc * 128, (lc + 1) * 128)
                bk = d
                S.mm([("matmul", dict(out=psb[bk][:, 0:128], lhsT=kT[:, tsl], rhs=qT[:, tsl], start=True, stop=True))],
                     r=[(KT, lc // 4), (QT, lc // 4)], w=[PSK(bk)])
                S.op("dve", "scalar_tensor_tensor", r=[PSK(bk), GK, CB], w=[("Sm", d, p)], out=Sm[:, d, p, :], in0=psb[bk][:, 0:128],
                     scalar=wkc, in1=maskap[d], op0=MUL, op1=MUL)
            S.op("pool", "tensor_scalar", r=[(KTOK, c), GK], w=[("kw", d, p)], out=kw[:, d, p, :], in0=ktok[:, c, :], scalar1=wkc,
                 scalar2=1.0, op0=MUL, op1=MUL)

        def stage_b(k, d):
            c = order[d][k]
            p = k % 2
            lat = c >= 2
            lc = c - 2
            rc = rbt[:, d, c, hd:hd + 1]
            ed = eden[:, d, c, hd:hd + 1]
            tsl = slice(lc * 128, (lc + 1) * 128)
            if lat:
                S.op("act", "activation", r=[("C32", d), GK], w=[("Cs", d)], out=Cs[:, d, 0:258], in_=C32[:, d, 0:258],
                     func=AF.Copy, scale=rc)
                S.mm([("matmul", dict(out=psb[2 + d][:, 0:258], lhsT=Sm[:, d, p, :], rhs=vext[:, c, 0:258], start=True, stop=False)),
                      ("matmul", dict(out=psb[2 + d][:, 0:258], lhsT=qT[:, tsl], rhs=Cs[:, d, 0:258], start=False, stop=True))],
                     r=[("Sm", d, p), (VX, c), (VX, "ones"), (QT, lc // 4), ("Cs", d)], w=[PSK(2 + d)])
            S.mm([("matmul", dict(out=psb[4 + d][:, 0:258], lhsT=kw[:, d, p, :], rhs=vext[:, c, 0:258], start=True, stop=True))],
                 r=[("kw", d, p), (VX, c), (VX, "ones")], w=[PSK(4 + d)])
            S.op("dve", "scalar_tensor_tensor", r=[("C32", d), PSK(4 + d), GK], w=[("C32", d)], out=C32[:, d, 0:258],
                 in0=C32[:, d, 0:258], scalar=rc, in1=psb[4 + d][:, 0:258], op0=MUL, op1=ADD)
            if lat:
                S.op("dve", "tensor_scalar", r=[PSK(2 + d), GK], w=[("den", d)], out=den[:, d, 0:1], in0=psb[2 + d][:, 256:257],
                     scalar1=-1.0, scalar2=ed, op0=MUL, op1=MAX)
                S.op("dve", "tensor_tensor", r=[PSK(2 + d), ("den", d)], w=[("den", d)], out=den[:, d, 0:1], in0=psb[2 + d][:, 256:257],
                     in1=den[:, d, 0:1], op=MAX)
                S.op("dve", "reciprocal", r=[("den", d)], w=[("den", d)], out=den[:, d, 1:2], in_=den[:, d, 0:1])
                first = (d == 0 and lc < 8) or (d == 1 and lc >= 8)
                if first:
                    S.op("act", "activation", r=[PSK(2 + d), ("den", d)], w=[("hbuf", lc)], out=hbuf[:, lc, :],
                         in_=psb[2 + d][:, 0:256], func=AF.Copy, scale=den[:, d, 1:2])
                else:
                    S.op("dve", "scalar_tensor_tensor", r=[PSK(2 + d), ("den", d), ("hbuf", lc)], w=[("hbuf", lc)],
                         out=hbuf[:, lc, :], in0=psb[2 + d][:, 0:256], scalar=den[:, d, 1:2], in1=hbuf[:, lc, :], op0=MUL, op1=ADD)
                    finalize(lc, d)

        for k in range(NT + 1):
            for d in range(2):
                if k < NT:
                    stage_a(k, d)
            for d in range(2):
                if k >= 1:
                    stage_b(k - 1, d)
    dump("mlT", mlT[:], [128, 8, L], [("mlT", ch, lc) for ch in range(8) for lc in range(16)])
    if stop_after <= 5:
        return finish()

    S.ph = 6
    PW.reset()
    wq_a = PW.alloc("wq_a", [128, 8, 256], BF16)
    wk_a = PW.alloc("wk_a", [128, 8, 128], BF16)
    cs = [PW.alloc(f"cs{i}", [128, 2, 512], F32) for i in range(2)]
    qTa = PW.alloc("qTa", [128, 2, L], BF16)
    kTlo = PW.alloc("kTlo", [128, TOK], BF16)
    kThi = PW.alloc("kThi", [128, TOK], BF16)
    vA = PW.alloc("vA", [128, NT, 128], BF16)
    vB = PW.alloc("vB", [128, NT, 128], BF16)
    sqs = [PW.alloc(f"sq{i}", [128, 512], BF16) for i in range(2)]
    rss = [PW.alloc(f"rs{i}", [128, 512], F32) for i in range(2)]
    qhs = [PW.alloc(f"qh{i}", [128, 512], F32) for i in range(2)]
    t2s = [PW.alloc(f"t2{i}", [128, 512], F32) for i in range(2)]
    Pt = PW.alloc("Pt", [128, 2, 5, 512], BF16)
    dsum = PW.alloc("dsum", [128, 2, 2, 512], F32)
    swf_t = PW.alloc("swf", [128, 128], BF16)
    drec = PW.alloc("drec", [128, 2, 2, 512], BF16)
    mneg = PW.alloc("mneg", [128, NCB2], BF16)
    S.dma("pool", w=[("mneg",)], out=mneg[:], in_=env["cb2_d"])
    S.dma("pool", w=[("swf",)], out=swf_t[:], in_=cf_d[:, C_SWF:C_SWF + 128])
    swapf = swf_t[:]
    S.op("dve", "memset", w=[("kTlo", "z")], ap=kTlo[64:128, :], constant=0.0)
    S.op("dve", "memset", w=[("kThi", "z")], ap=kThi[0:64, :], constant=0.0)
    S.op("dve", "memset", w=[("vA", "ones")], ap=vA[:, :, 64:128], constant=1.0)
    S.op("dve", "memset", w=[("vB", "ones")], ap=vB[:, :, 0:64], constant=1.0)
    pipe_n = [0]

    def qk_pipeline(wt, wkey, cols, tok0, n, wcol, rope, csb, dsts, dkeys):
        pi_ = pipe_n[0] % 2
        b0 = 3 * pi_
        pipe_n[0] += 1
        sq, rs, qh, t2 = sqs[pi_], rss[pi_], qhs[pi_], t2s[pi_]
        SQ, RS, QH, T2 = (f"sq{pi_}",), (f"rs{pi_}",), (f"qh{pi_}",), (f"t2{pi_}",)
        pq, pms, prot = b0, b0 + 1, b0 + 2
        rk = [("hT", t) for t in range(tok0 // 128, (tok0 + n) // 128)]
        S.mm([("matmul", dict(out=psb[pq][:, 0:n], lhsT=wt[:, kc, cols], rhs=hT[:, kc, tok0:tok0 + n],
                              start=(kc == 0), stop=(kc == 7))) for kc in range(8)], r=[wkey] + rk, w=[PSK(pq)])
        S.op("act", "activation", r=[PSK(pq)], w=[SQ], out=sq[:, 0:n], in_=psb[pq][:, 0:n], func=AF.Square)
        S.mm([("matmul", dict(out=psb[pms][:, 0:n], lhsT=bdiag, rhs=sq[:, 0:n], start=True, stop=True))], r=[SQ, CB], w=[PSK(pms)])
        S.op("act", "activation", r=[PSK(pms)], w=[RS], out=rs[:, 0:n], in_=psb[pms][:, 0:n], func=AF.Ln, bias=EPS)
        S.op("act", "activation", r=[RS], w=[RS], out=rs[:, 0:n], in_=rs[:, 0:n], func=AF.Exp, scale=-0.5)
        S.op("dve", "scalar_tensor_tensor", r=[PSK(pq), ("wq8",), RS], w=[QH], out=qh[:, 0:n], in0=psb[pq][:, 0:n],
             scalar=wcol, in1=rs[:, 0:n], op0=MUL, op1=MUL)
        if rope:
            S.op("pool", "tensor_copy", r=[QH], w=[SQ], out=sq[:, 0:n], in_=qh[:, 0:n])
            S.mm([("matmul", dict(out=psb[prot][:, 0:n], lhsT=permb, rhs=sq[:, 0:n], start=True, stop=True))], r=[SQ, CB], w=[PSK(prot)])
            S.op("dve", "tensor_tensor", r=[PSK(prot), csb[1]], w=[T2], out=t2[:, 0:n], in0=psb[prot][:, 0:n], in1=csb[0][:, 1, 0:n], op=MUL)
            S.op(COS_ENG, "tensor_tensor", r=[QH, csb[1]], w=[QH], out=qh[:, 0:n], in0=qh[:, 0:n], in1=csb[0][:, 0, 0:n], op=MUL)
            for (dst, p0, p1), dk in zip(dsts, dkeys):
                S.op("dve", "tensor_tensor", r=[QH, T2], w=[dk], out=dst, in0=qh[p0:p1, 0:n], in1=t2[p0:p1, 0:n], op=ADD)
        else:
            for (dst, p0, p1), dk in zip(dsts, dkeys):
                S.op("dve", "tensor_copy", r=[QH], w=[dk], out=dst, in_=qh[p0:p1, 0:n])

    for kv in range(4):
        wload(wq_a[:], win_d, O_AQ + kv * 256, 256, ("wq_a",))
        wload(wk_a[:, :, 0:64], win_d, O_AK + kv * 64, 64, ("wk_a",))
        wload(wk_a[:, :, 64:128], win_d, O_AK + kv * 64, 64, ("wk_a",))
        S.op("dve", "tensor_copy", r=[("vatt",)], w=[("vA",)], out=vA[:, :, 0:64], in_=vatt[:, :, kv * 64:(kv + 1) * 64])
        S.op("dve", "tensor_copy", r=[("vatt",)], w=[("vB",)], out=vB[:, :, 64:128], in_=vatt[:, :, kv * 64:(kv + 1) * 64])
        qk_pipeline(wk_a, ("wk_a",), slice(0, 128), 0, 256, wq8[:, 1:2], False, None,
                    [(kTlo[0:64, 0:256], 0, 64), (kThi[64:128, 0:256], 64, 128)], [("kTlo", 0), ("kThi", 0)])
        for tg in range(4):
            cb_ = cs[tg % 2]
            ck = (f"cs{tg % 2}",)
            S.dma("sp", w=[ck], out=cb_[:, 0, :], in_=cf_d[:, C_COS + tg * 512:C_COS + (tg + 1) * 512])
            S.dma("sp", w=[ck], out=cb_[:, 1, :], in_=cf_d[:, C_SIN + tg * 512:C_SIN + (tg + 1) * 512])
            t0 = 256 + tg * 512
            qk_pipeline(wk_a, ("wk_a",), slice(0, 128), t0, 512, wq8[:, 1:2], True, (cb_, ck),
                        [(kTlo[0:64, t0:t0 + 512], 0, 64), (kThi[64:128, t0:t0 + 512], 64, 128)], [("kTlo", 1 + tg), ("kThi", 1 + tg)])
            for e in range(2):
                qk_pipeline(wq_a, ("wq_a",), slice(e * 128, (e + 1) * 128), t0, 512, wq8[:, 0:1], True, (cb_, ck),
                            [(qTa[:, e, tg * 512:(tg + 1) * 512], 0, 128)], [("qTa", e, tg)])
        kall = [("kTlo", i) for i in range(5)] + [("kThi", i) for i in range(5)] + [("kTlo", "z"), ("kThi", "z")]
        sbanks = [0, 1, 6]
        scnt = [0]

        def att_front(qb):
            kts = []
            if qb > 0:
                kts.append((256 + (qb - 1) * 128, 2 + qb - 1, B_MGE))
            kts.append((256 + qb * 128, 2 + qb, None))
            if qb < 15:
                kts.append((256 + (qb + 1) * 128, 2 + qb + 1, B_MLE))
            kts.append((0, 0, None))
            kts.append((128, 1, None))
            u = qb % 2
            for i, (kc0, vt, mask) in enumerate(kts):
                bk = sbanks[scnt[0] % 3]
                scnt[0] += 1
                ins = []
                if mask is not None:
                    mcol = B_NGE if mask == B_MGE else B_NLE
                    ins.append(("matmul", dict(out=psb[bk][:], lhsT=identb, rhs=mneg[:, mcol:mcol + 512], start=True, stop=False)))
                ins += [("matmul", dict(out=psb[bk][:, g * 128:(g + 1) * 128], lhsT=(kTlo if g % 2 == 0 else kThi)[:, kc0:kc0 + 128],
                                        rhs=qTa[:, g // 2, qb * 128:(qb + 1) * 128], start=(mask is None), stop=(mask is None or g == 3)))
                        for g in range(4)]
                S.mm(ins, r=kall + [("qTa", e, qb // 4) for e in range(2)] + [("mneg",), CB], w=[PSK(bk)])
                S.op("act", "activation", r=[PSK(bk)], w=[("Pt", u, i)], out=Pt[:, u, i, :], in_=psb[bk][:], func=AF.Exp)
            return kts

        def att_back(qb, kts):
            nk = len(kts)
            u = qb % 2
            o1b, o2b = (2, 3) if qb % 2 == 0 else (4, 5)
            v4 = lambda ap: ap.rearrange("p (e h t) -> p e h t", e=2, h=2)
            S.mm([("matmul", dict(out=psb[o1b][:, 0:256], lhsT=vA[:, kts[i][1], :], rhs=v4(Pt[:, u, i, :])[:, :, 0, :],
                                  start=(i == 0), stop=(i == nk - 1)))
                  for i in range(nk)], r=[("Pt", u, i) for i in range(nk)] + [("vA",), ("vA", "ones")], w=[PSK(o1b)])
            S.mm([("matmul", dict(out=psb[o2b][:, 0:256], lhsT=vB[:, kts[i][1], :], rhs=v4(Pt[:, u, i, :])[:, :, 1, :],
                                  start=(i == 0), stop=(i == nk - 1)))
                  for i in range(nk)], r=[("Pt", u, i) for i in range(nk)] + [("vB",), ("vB", "ones")], w=[PSK(o2b)])
            v3 = lambda ap: ap.rearrange("p (e t) -> p e t", e=2)
            qsl = slice(qb * 128, (qb + 1) * 128)
            tgb = (qb // 4) % 2
            dsl = slice((qb % 4) * 128, (qb % 4 + 1) * 128)
            dk = ("dsum", tgb, qb % 4)
            S.op("dve", "tensor_copy", r=[PSK(o1b)], w=[("attT", kv, qb)], out=attT[0:64, 2 * kv:2 * kv + 2, qsl], in_=v3(psb[o1b][0:64, 0:256]))
            S.op("dve", "tensor_copy", r=[PSK(o1b)], w=[dk], out=dsum[64:128, tgb, :, dsl], in_=v3(psb[o1b][64:128, 0:256]))
            S.op("dve", "tensor_copy", r=[PSK(o2b)], w=[("attT", kv, qb)], out=attT[64:128, 2 * kv:2 * kv + 2, qsl], in_=v3(psb[o2b][64:128, 0:256]))
            S.op("dve", "tensor_copy", r=[PSK(o2b)], w=[dk], out=dsum[0:64, tgb, :, dsl], in_=v3(psb[o2b][0:64, 0:256]))
            if qb % 4 == 3:
                att_norm(qb // 4)

        def att_norm(tg):
            tgb = tg % 2
            dkeys = [("dsum", tgb, j) for j in range(4)]
            akeys = [("attT", kv, qb) for qb in range(tg * 4, tg * 4 + 4)]
            gsl = slice(tg * 512, (tg + 1) * 512)
            for e in range(2):
                for hf in range(2):
                    hcol = kv * 4 + 2 * e + hf
                    psl = slice((1 - hf) * 64, (2 - hf) * 64)
                    S.op("dve", "tensor_scalar", r=dkeys + [("sexp",)], w=dkeys, out=dsum[psl, tgb, e, :], in0=dsum[psl, tgb, e, :],
                         scalar1=sexp[psl, hcol:hcol + 1], scalar2=None, op0=ADD)
            rkeys = [("drec", tgb)]
            S.op("dve", "reciprocal", r=dkeys, w=dkeys, out=dsum[:, tgb], in_=dsum[:, tgb])
            ni = S.op("act", "copy", r=dkeys, w=rkeys, out=drec[:, tgb], in_=dsum[:, tgb])
            S.nodes[ni]["lag"] = NORM_SLACK
            for e in range(2):
                bk = 7
                S.mm([("matmul", dict(out=psb[bk][:], lhsT=swapf, rhs=drec[:, tgb, e, :], start=True, stop=True))], r=rkeys + [("swf",)], w=[PSK(bk)])
                S.op("dve", "tensor_tensor", r=akeys + [PSK(bk)], w=akeys, out=attT[:, 2 * kv + e, gsl], in0=attT[:, 2 * kv + e, gsl],
                     in1=psb[bk][:], op=MUL)

        prev = None
        for qb in range(16):
            kts = att_front(qb)
            if prev is not None:
                att_back(*prev)
            prev = (qb, kts)
        att_back(*prev)
    dump("attT", attT[:], [128, 8, L], [("attT", kv, qb) for kv in range(4) for qb in range(16)])
    if stop_after <= 6:
        return finish()

    S.ph = 7
    PW.reset()
    P7A = Pool(nc, S, "p7a", o3, o3 + 16384)
    P7B = Pool(nc, S, "p7b", o3 + 16384, SBUF_LIMIT)
    P7A.pending = set(PW.pending)
    P7B.pending = set(PW.pending)
    wch = [P7A.alloc(f"wch{i}", [128, 4, 8, 128], BF16) for i in range(2)]
    wout = P7B.alloc("wout", [128, 8, 1024], BF16)
    ymT = P7B.alloc("ymT", [128, 8, L], BF16)
    sga = [P7B.alloc(f"sga{i}", [128, 512], F32) for i in range(2)]
    sgm = [P7B.alloc(f"sgm{i}", [128, 512], F32) for i in range(2)]
    xt = [P7B.alloc(f"xt{i}", [128, 1024], F32) for i in range(2)]
    wload(wout[:], wout_d, 0, 1024, ("wout",))
    for kc in range(8):
        S.op("pool", "tensor_tensor", r=[("wout",), ("g1b",)], w=[("wout",)], out=wout[:, kc, :], in0=wout[:, kc, :], in1=g1b[:], op=MUL)
    att_keys = [("attT", kv, qb) for kv in range(4) for qb in range(16)]
    ml_keys = [("mlT", ch, lc) for ch in range(8) for lc in range(16)]
    it = 0
    for oc in range(8):
        bi = oc % 2
        w_ = wch[bi]
        wk_ = (f"wch{bi}",)
        wload(w_[:, 0], win_d, O_GA + oc * 128, 128, wk_)
        wload(w_[:, 1], win_d, O_GM + oc * 128, 128, wk_)
        wload(w_[:, 2], wba_d, oc * 128, 128, wk_)
        wload(w_[:, 3], wbm_d, oc * 128, 128, wk_)
        for tg in range(4):
            tok0 = 256 + tg * 512
            rk = [("hT", 2 + tg * 4 + i) for i in range(4)]
            pb0 = 4 * (it % 2)
            sb = it % 2
            it += 1
            srcs = [(hT, tok0, rk), (hT, tok0, rk), (attT, tg * 512, att_keys), (mlT, tg * 512, ml_keys)]
            for j, (src, c0, keys) in enumerate(srcs):
                S.mm([("matmul", dict(out=psb[pb0 + j][:], lhsT=w_[:, j, kc, :], rhs=src[:, kc, c0:c0 + 512],
                                      start=(kc == 0), stop=(kc == 7))) for kc in range(8)], r=[wk_] + keys, w=[PSK(pb0 + j)])
            ka, km = (f"sga{sb}",), (f"sgm{sb}",)
            S.op("act", "activation", r=[PSK(pb0)], w=[ka], out=sga[sb][:], in_=psb[pb0][:], func=AF.Sigmoid)
            S.op("act", "activation", r=[PSK(pb0 + 1)], w=[km], out=sgm[sb][:], in_=psb[pb0 + 1][:], func=AF.Sigmoid)
            S.op("dve", "tensor_tensor", r=[PSK(pb0 + 2), ka], w=[ka], out=sga[sb][:], in0=psb[pb0 + 2][:], in1=sga[sb][:], op=MUL)
            S.op("dve", "tensor_tensor", r=[PSK(pb0 + 3), km], w=[km], out=sgm[sb][:], in0=psb[pb0 + 3][:], in1=sgm[sb][:], op=MUL)
            S.op("pool", "tensor_tensor", r=[ka, km], w=[("ymT", oc, tg)], out=ymT[:, oc, tg * 512:(tg + 1) * 512], in0=sga[sb][:], in1=sgm[sb][:], op=ADD)
    PA.reset()
    P7A.reset()
    h2T = PA.alloc("h2T", [128, 8, L], BF16)
    xn7 = [P7A.alloc(f"xn{i}", [128, 1024], F32) for i in range(2)]
    junk7 = P7A.alloc("junk", [128, 1024], BF16)
    st7 = P7A.alloc("st4", [128, 2, 4], F32)
    for tile in range(16):
        bi = tile % 2
        xk = (f"xt{bi}",)
        S.dma("sp", w=[xk], out=xt[bi][:], in_=x_d[tile * 128:(tile + 1) * 128, :])
        for nb_ in range(2):
            bk = 2 * bi + nb_
            S.mm([("matmul", dict(out=psb[bk][:], lhsT=ymT[:, kc, tile * 128:(tile + 1) * 128], rhs=wout[:, kc, nb_ * 512:(nb_ + 1) * 512],
                                  start=(kc == 0), stop=(kc == 7))) for kc in range(8)], r=[("ymT", oc, tile // 4) for oc in range(8)] + [("wout",)],
                 w=[PSK(bk)])
            S.op("dve", "tensor_tensor", r=[PSK(bk), xk], w=[xk], out=xt[bi][:, nb_ * 512:(nb_ + 1) * 512], in0=psb[bk][:],
                 in1=xt[bi][:, nb_ * 512:(nb_ + 1) * 512], op=ADD)
        S.dma("sp", r=[xk], w=[("xmid", tile)], out=xmid_d[tile * 128:(tile + 1) * 128, :], in_=xt[bi][:])
        norm_tile(junk7, xk, xt[bi][:], xn7[bi][:], (f"xn{bi}",), st7[:, bi, :], ("st4", bi), 1024)
        to_featmajor(xn7[bi], (f"xn{bi}",), h2T, ("h2T", tile), tile * 128, 4, 5, (4 + 2 * bi, 5 + 2 * bi))
    if debug:
        t = dbg_out("xmid", [L, D])
        S.dma("sp", r=[("xmid", i) for i in range(16)], w=[("dbg", "xmid")], out=t, in_=xmid_d)
    if stop_after <= 7:
        return finish()

    S.ph = 8
    PC.reset()
    PW.pending = PW.pending | S.events_of(P7A.names + P7B.names) | P7A.pending | P7B.pending
    PW.reset()
    wua = PC.alloc("wua", [128, 8, 1408], BF16)
    wug = PC.alloc("wug", [128, 8, 1408], BF16)
    wdn = PW.alloc("wdn", [128, 11, 1024], BF16)
    actT = PW.alloc("actT", [128, 11, 384], BF16)
    accA = [PW.alloc(f"accA{i}", [128, 384], F32) for i in range(3)]
    accG = [PW.alloc(f"accG{i}", [128, 384], F32) for i in range(3)]
    ltap = [PW.alloc(f"ltap{i}", [128, 384], F32) for i in range(3)]
    xin = [PW.alloc(f"xin{i}", [128, 1024], F32) for i in range(4)]
    dump("h2T", h2T[:], [128, 8, L], [("h2T", t) for t in range(16)])
    h2keys = [("h2T", t) for t in range(16)]
    for half in range(2):
        for ii in range(11):
            i = half * 11 + ii
            if ii % 4 == 0:
                nc_ = min(4, 11 - ii) * 128
                wload(wua[:, :, ii * 128:ii * 128 + nc_], wup_d, i * 128, nc_, ("wua", ii // 4))
                wload(wug[:, :, ii * 128:ii * 128 + nc_], wup_d, DFF + i * 128, nc_, ("wug", ii // 4))
            wload(wdn[:, ii:ii + 1, :], wdn_d, 0, 1024, ("wdn", ii), kcs=1, row0=i * 128)
            S.op("dve", "tensor_tensor", r=[("wdn", ii), ("g2b",)], w=[("wdn", ii)], out=wdn[:, ii, :], in0=wdn[:, ii, :], in1=g2b[:], op=MUL)
        for w in range(6):
            lo = w * 384
            hi = min(L, lo + 384)
            n = hi - lo
            il = max(lo - 1, 0)
            ih = min(hi + 1, L)
            nin = ih - il
            off = lo - il
            h2keys = [("h2T", t) for t in range(il // 128, (ih - 1) // 128 + 1)]
            for ii in range(11):
                i = half * 11 + ii
                ab = ii % 3
                pa, pg = 2 * ab, 2 * ab + 1
                S.mm([("matmul", dict(out=psb[pa][:, 0:nin], lhsT=wua[:, kc, ii * 128:(ii + 1) * 128], rhs=h2T[:, kc, il:ih],
                                      start=(kc == 0), stop=(kc == 7))) for kc in range(8)], r=[("wua", ii // 4)] + h2keys, w=[PSK(pa)])
                S.mm([("matmul", dict(out=psb[pg][:, 0:nin], lhsT=wug[:, kc, ii * 128:(ii + 1) * 128], rhs=h2T[:, kc, il:ih],
                                      start=(kc == 0), stop=(kc == 7))) for kc in range(8)], r=[("wug", ii // 4)] + h2keys, w=[PSK(pg)])
                for (pb_, acc, akey, ch) in ((pa, accA[ab], (f"accA{ab}",), i), (pg, accG[ab], (f"accG{ab}",), 22 + i)):
                    cw = lambda j: vecs[:, V_CW + j * 44 + ch:V_CW + j * 44 + ch + 1]
                    S.op("act", "activation", r=[PSK(pb_), VK], w=[akey], out=acc[:, 0:n], in_=psb[pb_][:, off:off + n], func=AF.Identity,
                         scale=cw(1), bias=vecs[:, V_CB + ch:V_CB + ch + 1])
                    j0 = 1 - off
                    if FFN_SPLIT and pb_ == pa:
                        lk = (f"ltap{ab}",)
                        S.op("act", "activation", r=[PSK(pb_), VK], w=[lk], out=ltap[ab][:, j0:n], in_=psb[pb_][:, off + j0 - 1:off + n - 1],
                             func=AF.Copy, scale=cw(0))
                        S.op("pool", "tensor_tensor", r=[lk, akey], w=[akey], out=acc[:, j0:n], in0=acc[:, j0:n], in1=ltap[ab][:, j0:n], op=ADD)
                    else:
                        S.op("dve", "scalar_tensor_tensor", r=[PSK(pb_), VK, akey], w=[akey], out=acc[:, j0:n],
                             in0=psb[pb_][:, off + j0 - 1:off + n - 1], scalar=cw(0), in1=acc[:, j0:n], op0=MUL, op1=ADD)
                    j1 = min(n, nin - 1 - off)
                    S.op("dve", "scalar_tensor_tensor", r=[PSK(pb_), VK, akey], w=[akey], out=acc[:, 0:j1],
                         in0=psb[pb_][:, off + 1:off + 1 + j1], scalar=cw(2), in1=acc[:, 0:j1], op0=MUL, op1=ADD)
                S.op("act", "activation", r=[(f"accG{ab}",)], w=[(f"accG{ab}",)], out=accG[ab][:, 0:n], in_=accG[ab][:, 0:n], func=AF.Silu)
                S.op("pool", "tensor_tensor", r=[(f"accG{ab}",), (f"accA{ab}",)], w=[("actT", ii)], out=actT[:, ii, 0:n],
                     in0=accG[ab][:, 0:n], in1=accA[ab][:, 0:n], op=MUL)
            for m in range(n // 128):
                tile = lo // 128 + m
                bi = tile % 4
                xk = (f"xin{bi}",)
                src = xmid_d if half == 0 else out_d
                skey = ("xmid", tile) if half == 0 else ("outd", tile)
                S.dma("sp", r=[skey], w=[xk], out=xin[bi][:], in_=src[tile * 128:(tile + 1) * 128, :])
                for nb_ in range(2):
                    bk = 6 + nb_
                    S.mm([("matmul", dict(out=psb[bk][:], lhsT=actT[:, ii, m * 128:(m + 1) * 128], rhs=wdn[:, ii, nb_ * 512:(nb_ + 1) * 512],
                                          start=(ii == 0), stop=(ii == 10))) for ii in range(11)],
                         r=[("actT", ii) for ii in range(11)] + [("wdn", ii) for ii in range(11)], w=[PSK(bk)])
                    S.op("dve", "tensor_tensor", r=[PSK(bk), xk], w=[xk], out=xin[bi][:, nb_ * 512:(nb_ + 1) * 512], in0=psb[bk][:],
                         in1=xin[bi][:, nb_ * 512:(nb_ + 1) * 512], op=ADD)
                S.dma("sp", r=[xk], w=[("outd", tile)], out=out_d[tile * 128:(tile + 1) * 128, :], in_=xin[bi][:])
    finish()


_CACHE = {}


def _run(inputs, debug=False, core_ids=None, stop_after=99):
    key = ("prog", debug, stop_after)
    if key not in _CACHE:
        _CACHE[key] = build_program(debug, stop_after)
    nc, dbg_names, S = _CACHE[key]
    cf, cb, cb2 = host_constants()
    f = lambda a: np.ascontiguousarray(np.asarray(a, dtype=np.float32))
    x, c, ctx, c_ctx = f(inputs["x"]), f(inputs["c"]), f(inputs["ctx"]), f(inputs["c_ctx"])
    shared = {
        "bmod_row": f(inputs["b_mod"]).reshape(1, 6144), "cstf": cf, "cstb": cb, "cstb2": cb2,
        "w_mod": f(inputs["w_mod"])[0], "w_in": f(inputs["w_in"])[0], "w_ba": f(inputs["w_branch_att"])[0],
        "w_bm": f(inputs["w_branch_ml"])[0], "w_out": f(inputs["w_out"])[0], "w_up": f(inputs["w_up"])[0],
        "w_down": f(inputs["w_down"])[0],
    }
    cores = list(range(8)) if core_ids is None else core_ids
    in_maps = []
    for b in cores:
        m = dict(shared)
        m["x"] = x[b]
        m["ctx"] = ctx[b]
        m["vecs"] = host_vecs(b, c, c_ctx, f(inputs["b_mod"]), f(inputs["norm1_w"]), f(inputs["norm2_w"]),
                              f(inputs["ml_norm_w"]), f(inputs["conv_w"]), f(inputs["conv_b"]), f(inputs["q_norm_w"]),
                              f(inputs["k_norm_w"]), f(inputs["ml_gate_b"]), f(inputs["attn_sink"]))
        in_maps.append(m)
    res = run_bass_kernel_spmd(nc, in_maps, core_ids=list(range(len(cores))))
    return res, dbg_names


def kernel(**inputs):
    res, _ = _run(inputs)
    return np.stack([np.asarray(r["out"], dtype=np.float32) for r in res.results], axis=0)
```

```python
import numpy as np
import concourse.bass as bass
import concourse.mybir as mybir
from concourse.bass_utils import run_bass_kernel_spmd

F32 = mybir.dt.float32
BF16 = mybir.dt.bfloat16
U8 = mybir.dt.uint8
AF = mybir.ActivationFunctionType
ALU = mybir.AluOpType
AX = mybir.AxisListType

D = 1024
L = 2048
C = 256
NT = 18
TOK = NT * 128
IN_W = 6672
DFF = 2816
EPS = 1e-6
O_AQ, O_AK, O_AV, O_MQ, O_MK, O_MV, O_MO, O_MG, O_GA, O_GM = 0, 1024, 1280, 1536, 2048, 2560, 3584, 4608, 4624, 5648

C_IDF, C_TRF, C_TRB, C_ONF, C_PMF, C_SWF, C_COS, C_SIN, NCF = 0, 128, 256, 384, 512, 640, 768, 2816, 4864
B_IDB, B_MLE, B_MGE, B_ONB, B_BD, B_PM, NCB = 0, 128, 640, 1152, 1280, 1408, 1536
B_NLE, B_NGE, NCB2 = 0, 512, 1024
V_N1W, V_N2W, V_BMOD, V_CC, V_CCX, V_MLNW, V_CW, V_CB, V_QNW, V_KNW, V_GB, V_SINK, NV = \
    0, 8, 16, 64, 72, 80, 88, 220, 264, 265, 266, 282, 298

SBUF_BASE = 16512
SBUF_LIMIT = 229312
SAME_ENGINE_SYNC = True
FMB = 6
REORDER = True
LOOKAHEAD = 0.1
FFN_SPLIT = True
PE_MARGIN = 0.0
NORM_SLACK = 0.0
MASK_ENG = "dve"
COS_ENG = "pool"


def _free_elems(ap):
    n = 1
    for v in ap.shape[1:]:
        n *= int(v)
    return n


class Sched:
    ENGS = ("pe", "act", "dve", "pool", "sp")

    def __init__(self, nc, eng_sems, dma_sems):
        self.nc = nc
        self.sem = dict(eng_sems)
        self.dma_sems = dma_sems
        self.nodes = []
        self.lastw = {}
        self.readers = {}
        self.base = {}
        self.ninst = 0
        self.prog = {e: [] for e in self.ENGS}

    @staticmethod
    def _excl(r, w):
        pr = [x for x in r if x[0] == "ps"]
        if not pr:
            return r, w
        return [x for x in r if x[0] != "ps"], list(w) + [x for x in pr if x not in w]

    def _add(self, eng, kind, insts, r, w, dur):
        r, w = self._excl(r, w)
        deps = set()
        for x in r:
            if x in self.lastw:
                deps.add(self.lastw[x])
            b = self.base.get(x[0])
            if b:
                deps |= b
        for x in w:
            if x in self.lastw:
                deps.add(self.lastw[x])
            deps.update(self.readers.get(x, ()))
            b = self.base.get(x[0])
            if b:
                deps |= b
        idx = len(self.nodes)
        self.nodes.append(dict(eng=eng, kind=kind, insts=insts, deps=deps, dur=dur, ph=getattr(self, "ph", 0)))
        for x in r:
            self.readers.setdefault(x, []).append(idx)
        for x in w:
            self.lastw[x] = idx
            self.readers[x] = []
        self.ninst += len(insts)
        return idx

    def op(self, e, method, r=(), w=(), **kw):
        n = _free_elems(kw["out"]) if "out" in kw else _free_elems(kw["ap"])
        from_psum = any(str(getattr(kw.get(k), "space", "")).upper().find("PSUM") >= 0 or "PSum" in str(type(getattr(kw.get(k), "tensor", None)))
                        for k in ("in_", "in0", "in1"))
        rate = {"act": 0.70e3 if from_psum else 0.92e3, "dve": 0.72e3 if from_psum else 0.90e3, "pool": 0.36e3}[e]
        dur = max(64, n) / rate + 0.1
        if "accum_out" in kw:
            dur += 0.1
        return self._add(e, "op", [(method, kw)], r, w, dur)

    def mm(self, insts, r=(), w=()):
        dur = 0.05
        for m, kw in insts:
            if m == "transpose":
                n = max(64, int(kw["in_"].shape[0]))
                f = 1.0
            else:
                n = max(64, _free_elems(kw["rhs"]))
                f = 4.0 if kw["rhs"].dtype == F32 else 1.0
            dur += f * n / 2.2e3 + 0.02
        return self._add("pe", "mm", insts, r, w, dur)

    def dma(self, e, r=(), w=(), **kw):
        nbytes = _free_elems(kw["out"]) * int(kw["out"].shape[0]) * 4
        dur = 2.0 + nbytes / 180e3
        return self._add(e, "dma", [("dma_start", kw)], r, w, dur)

    def events_of(self, names):
        out = set()
        names = set(names)
        for k, v in self.lastw.items():
            if k[0] in names:
                out.add(v)
        for k, lst in self.readers.items():
            if k[0] in names:
                out.update(lst)
        for n in names:
            out |= self.base.get(n, set())
        return out

    def wait_all(self, e):
        pass

    def schedule_emit(self, reorder=True):
        import heapq
        nodes = self.nodes
        N = len(nodes)
        LAT_X, LAT_S = 0.15, 0.08
        succs = [[] for _ in range(N)]
        indeg = [0] * N
        for i, nd in enumerate(nodes):
            nd["deps"] = {d for d in nd["deps"] if d != i}
            indeg[i] = len(nd["deps"])
            for d in nd["deps"]:
                succs[d].append(i)
        fin = [0.0] * N
        start = [0.0] * N
        order = []
        if reorder:
            DELTA = LOOKAHEAD
            bl = [0.0] * N
            for i in range(N - 1, -1, -1):
                m = 0.0
                for j in succs[i]:
                    v = bl[j] + LAT_X
                    if v > m:
                        m = v
                bl[i] = nodes[i]["dur"] + m
            ready_t = [0.0] * N
            prim = {e: [] for e in self.ENGS}
            sec = {e: [] for e in self.ENGS}
            free = {e: 0.0 for e in self.ENGS}
            for i in range(N):
                if indeg[i] == 0:
                    heapq.heappush(prim[nodes[i]["eng"]], (0.0, i))
            done = 0
            while done < N:
                best = None
                for e in self.ENGS:
                    if sec[e]:
                        c = free[e]
                    elif prim[e]:
                        c = max(free[e], prim[e][0][0])
                    else:
                        continue
                    if best is None or c < best[0]:
                        best = (c, e)
                c, e = best
                p, s_ = prim[e], sec[e]
                hz = c + DELTA
                while p and p[0][0] <= hz:
                    rt, i = heapq.heappop(p)
                    heapq.heappush(s_, (-bl[i], i))
                _, i = heapq.heappop(s_)
                nd = nodes[i]
                st = max(free[e], ready_t[i])
                start[i] = st
                if nd["kind"] == "dma":
                    free[e] = st + 0.15
                    fin[i] = st + nd["dur"]
                else:
                    free[e] = st + nd["dur"]
                    fin[i] = free[e]
                order.append(i)
                done += 1
                for j in succs[i]:
                    lat = LAT_S if nodes[j]["eng"] == e and nd["kind"] != "dma" else LAT_X
                    if nodes[j]["eng"] == "pe" and e != "pe":
                        lat += PE_MARGIN
                    t = fin[i] + lat + nd.get("lag", 0.0)
                    if t > ready_t[j]:
                        ready_t[j] = t
                    indeg[j] -= 1
                    if indeg[j] == 0:
                        heapq.heappush(prim[nodes[j]["eng"]], (ready_t[j], j))
            self.est_us = max(fin) if N else 0.0
            self.sim_start, self.sim_fin = start, fin
            order.sort(key=lambda i: (start[i], i))
        else:
            order = list(range(N))
        cnt = {e: 0 for e in self.ENGS}
        waited = {e: {} for e in self.ENGS}
        dma_val = {e: [0] * len(v) for e, v in self.dma_sems.items()}
        dma_next = {e: 0 for e in self.dma_sems}
        ev = [None] * N
        prog = self.prog

        def semh(key):
            return self.dma_sems[key[1]][key[2]] if isinstance(key, tuple) else self.sem[key]

        def wait(e, key, val):
            if key == e and not SAME_ENGINE_SYNC:
                return
            if waited[e].get(key, 0) >= val:
                return
            waited[e][key] = val
            prog[e].append(lambda eng, sem=semh(key), val=val: eng.wait_ge(sem, val))

        for i in order:
            nd = nodes[i]
            e = nd["eng"]
            if nd["kind"] == "dma":
                slots = self.dma_sems[e]
                sl = dma_next[e]
                dma_next[e] = (sl + 1) % len(slots)
                key = ("d", e, sl)
                if dma_val[e][sl] > 0:
                    wait(e, key, dma_val[e][sl])
            need = {}
            for d in nd["deps"]:
                k, v = ev[d]
                if need.get(k, 0) < v:
                    need[k] = v
            for k, v in need.items():
                wait(e, k, v)
            if nd["kind"] == "dma":
                dma_val[e][sl] += 16
                kw = nd["insts"][0][1]
                prog[e].append(lambda eng, kw=kw, sem=slots[sl]: eng.dma_start(**kw).then_inc(sem, 16))
                ev[i] = (key, dma_val[e][sl])
            else:
                cnt[e] += 1
                sem = self.sem[e]
                n = len(nd["insts"])
                for j, (m, kw) in enumerate(nd["insts"]):
                    if j == n - 1:
                        prog[e].append(lambda eng, m=m, kw=kw, sem=sem: getattr(eng, m)(**kw).then_inc(sem, 1))
                    else:
                        prog[e].append(lambda eng, m=m, kw=kw: getattr(eng, m)(**kw))
                ev[i] = (e, cnt[e])
        for e in self.ENGS:
            for k in self.ENGS:
                if cnt[k] > 0:
                    wait(e, k, cnt[k])
            for de, vals in dma_val.items():
                for sl, v in enumerate(vals):
                    if v > 0:
                        wait(e, ("d", de, sl), v)


class Pool:
    def __init__(self, nc, S, name, lo, hi):
        self.nc, self.S, self.name, self.lo, self.hi = nc, S, name, lo, hi
        self.ptr = lo
        self.names = []
        self.pending = set()
        self.gen = 0

    def reset(self):
        self.pending = self.S.events_of(self.names) | self.pending
        self.names = []
        self.ptr = self.lo
        self.gen += 1

    def alloc(self, name, shape, dtype):
        esz = 4 if dtype == F32 else (2 if dtype == BF16 else 1)
        n = 1
        for s in shape[1:]:
            n *= s
        nbytes = (n * esz + 31) // 32 * 32
        assert self.ptr + nbytes <= self.hi, f"pool {self.name} overflow allocating {name}: {self.ptr + nbytes - self.hi} over"
        uname = f"{name}_{self.name}{self.gen}"
        t = self.nc.alloc_sbuf_tensor_at(uname, list(shape), dtype, offset=self.ptr)
        self.ptr += nbytes
        self.names.append(name)
        self.S.base[name] = set(self.pending)
        return t


def host_constants():
    cf = np.zeros((128, NCF), np.float32)
    cb = np.zeros((128, NCB), np.float32)
    idx = np.arange(128)
    cf[:, C_IDF:C_IDF + 128] = np.eye(128, dtype=np.float32)
    le = (idx[:, None] <= idx[None, :]).astype(np.float32)
    ge = (idx[:, None] >= idx[None, :]).astype(np.float32)
    cf[:, C_TRF:C_TRF + 128] = le
    cf[:, C_TRB:C_TRB + 128] = ge
    cf[:, C_ONF:C_ONF + 128] = 1.0
    partner = np.where((idx % 64) < 32, idx + 32, idx - 32)
    pm = np.zeros((128, 128), np.float32)
    pm[partner, idx] = 1.0
    cf[:, C_PMF:C_PMF + 128] = pm
    sw = np.zeros((128, 128), np.float32)
    sw[(idx + 64) % 128, idx] = 1.0
    cf[:, C_SWF:C_SWF + 128] = sw
    t = np.arange(L)
    row = (t // 64).astype(np.float32)
    col = (t % 64).astype(np.float32)
    inv_freq = (np.float32(10000.0) ** (-np.arange(16, dtype=np.float32) / np.float32(16))).astype(np.float32)
    ang = np.concatenate([row[:, None] * inv_freq[None, :], col[:, None] * inv_freq[None, :]], axis=1).astype(np.float32)
    cosv = np.cos(ang).astype(np.float32)
    sinv = np.sin(ang).astype(np.float32)
    j = idx % 32
    half = (idx % 64) // 32
    cf[:, C_COS:C_COS + L] = cosv[:, j].T
    sgn = np.where(half == 0, -1.0, 1.0).astype(np.float32)
    cf[:, C_SIN:C_SIN + L] = sinv[:, j].T * sgn[:, None]
    cb[:, B_IDB:B_IDB + 128] = np.eye(128, dtype=np.float32)
    cb[:, B_MLE:B_MLE + 512] = np.tile(le, (1, 4))
    cb[:, B_MGE:B_MGE + 512] = np.tile(ge, (1, 4))
    cb[:, B_ONB:B_ONB + 128] = 1.0
    bd = np.zeros((128, 128), np.float32)
    bd[:64, :64] = 1.0 / 64
    bd[64:, 64:] = 1.0 / 64
    cb[:, B_BD:B_BD + 128] = bd
    cb[:, B_PM:B_PM + 128] = pm
    cb2 = np.zeros((128, NCB2), np.float32)
    cb2[:, B_NLE:B_NLE + 512] = (np.tile(le, (1, 4)) - 1.0) * 30000.0
    cb2[:, B_NGE:B_NGE + 512] = (np.tile(ge, (1, 4)) - 1.0) * 30000.0
    return cf, cb, cb2


def host_vecs(b, c, c_ctx, b_mod, norm1_w, norm2_w, ml_norm_w, conv_w, conv_b, q_norm_w, k_norm_w, ml_gate_b, attn_sink):
    v = np.zeros((128, NV), np.float32)

    def pc(vec):
        return np.ascontiguousarray(vec.reshape(-1, 128).T)

    v[:, V_N1W:V_N1W + 8] = pc(norm1_w[0])
    v[:, V_N2W:V_N2W + 8] = pc(norm2_w[0])
    v[:, V_BMOD:V_BMOD + 48] = pc(b_mod[0])
    v[:, V_CC:V_CC + 8] = pc(c[b])
    v[:, V_CCX:V_CCX + 8] = pc(c_ctx)
    v[:, V_MLNW:V_MLNW + 8] = pc(ml_norm_w[0])
    for j in range(3):
        v[:, V_CW + j * 44:V_CW + (j + 1) * 44] = pc(conv_w[0, j])
    v[:, V_CB:V_CB + 44] = pc(conv_b[0])
    v[:, V_QNW] = np.tile(q_norm_w[0], 2)
    v[:, V_KNW] = np.tile(k_norm_w[0], 2)
    v[:, V_GB:V_GB + 16] = ml_gate_b[0].reshape(1, 16)
    v[:, V_SINK:V_SINK + 16] = attn_sink[0].reshape(1, 16)
    return v


def build_program(debug=False, stop_after=99):
    nc = bass.Bass("TRN2", target_bir_lowering=False)

    def din(name, shape):
        return nc.dram_tensor(name, list(shape), F32, kind="ExternalInput").ap()

    x_d = din("x", [L, D])
    ctx_d = din("ctx", [C, D])
    vecs_d = din("vecs", [128, NV])
    bmod_d = din("bmod_row", [1, 6144])
    cf_d = din("cstf", [128, NCF])
    cb_d = din("cstb", [128, NCB])
    cb2_d = din("cstb2", [128, NCB2])
    wmod_d = din("w_mod", [D, 6144])
    win_d = din("w_in", [D, IN_W])
    wba_d = din("w_ba", [D, D])
    wbm_d = din("w_bm", [D, D])
    wout_d = din("w_out", [D, D])
    wup_d = din("w_up", [D, 2 * DFF])
    wdn_d = din("w_down", [DFF, D])
    out_d = nc.dram_tensor("out", [L, D], F32, kind="ExternalOutput").ap()
    xmid_d = nc.dram_tensor("xmid_scr", [L, D], F32, kind="Internal").ap()
    dbg = {}

    def dbg_out(name, shape):
        if not debug:
            return None
        t = nc.dram_tensor("dbg_" + name, list(shape), F32, kind="ExternalOutput").ap()
        dbg[name] = t
        return t

    arena = nc.alloc_sbuf_tensor("arena", [128, SBUF_LIMIT - SBUF_BASE], U8)
    psb = [nc.alloc_psum_tensor(f"psb{i}", [128, 512], F32) for i in range(8)]

    def PS(i):
        return ("ps", i)

    import contextlib
    with contextlib.ExitStack() as es:
        eng_sems = {e: es.enter_context(nc.semaphore("s_" + e)) for e in Sched.ENGS}
        dma_sems = {"sp": [es.enter_context(nc.semaphore(f"dsp{i}")) for i in range(12)],
                    "pool": [es.enter_context(nc.semaphore(f"dpl{i}")) for i in range(24)]}
        S = Sched(nc, eng_sems, dma_sems)
        emit(nc, S, locals())
        S.schedule_emit(reorder=REORDER)
        block = es.enter_context(nc.Block())

        @block.tensor
        def _(eng):
            for f in S.prog["pe"]:
                f(eng)

        @block.scalar
        def _(eng):
            for f in S.prog["act"]:
                f(eng)

        @block.vector
        def _(eng):
            for f in S.prog["dve"]:
                f(eng)

        @block.gpsimd
        def _(eng):
            for f in S.prog["pool"]:
                f(eng)

        @block.sync
        def _(eng):
            for f in S.prog["sp"]:
                f(eng)
    return nc, list(dbg.keys()), S


def wview(w_d, c0, n):
    return w_d.rearrange("(kc p) n -> p kc n", p=128)[:, :, c0:c0 + n]


def emit(nc, S, env):
    x_d, ctx_d, vecs_d, bmod_d, cf_d, cb_d = env["x_d"], env["ctx_d"], env["vecs_d"], env["bmod_d"], env["cf_d"], env["cb_d"]
    wmod_d, win_d, wba_d, wbm_d, wout_d, wup_d, wdn_d = (env[k] for k in ("wmod_d", "win_d", "wba_d", "wbm_d", "wout_d", "wup_d", "wdn_d"))
    out_d, xmid_d, psb, dbg_out, debug = env["out_d"], env["xmid_d"], env["psb"], env["dbg_out"], env["debug"]
    stop_after = env.get("stop_after", 99)
    PSK = lambda i: ("ps", i)
    MUL, ADD, SUB, MAX = ALU.mult, ALU.add, ALU.subtract, ALU.max

    B0 = SBUF_BASE
    o1 = B0 + 26880
    o2 = o1 + 36864
    o3 = o2 + 65536
    PERS = Pool(nc, S, "pers", B0, o1)
    PA = Pool(nc, S, "pa", o1, o2)
    PC = Pool(nc, S, "pc", o2, o3)
    PW = Pool(nc, S, "pw", o3, SBUF_LIMIT)

    def wload(dst, w_d, c0, n, key, kcs=8, row0=0):
        src = w_d[row0:row0 + kcs * 128, :].rearrange("(kc p) n -> p kc n", p=128)[:, :, c0:c0 + n]
        return S.dma("pool", w=[key], out=dst, in_=src, max_dma_last_dim=8192)

    def dump(name, ap, shape, r):
        if not debug:
            return
        t = dbg_out(name, shape)
        S.dma("pool", r=r, w=[("dbg", name)], out=t, in_=ap, max_dma_last_dim=2048)

    def finish():
        for e in ("sp", "pe", "act", "dve", "pool"):
            S.wait_all(e)

    cstf = PERS.alloc("cstf", [128, C_SWF], F32)
    cstb = PERS.alloc("cstb", [128, NCB], BF16)
    vecs = PERS.alloc("vecs", [128, NV], F32)
    modT = PERS.alloc("modT", [128, 48, 2], F32)
    aff = PERS.alloc("aff", [128, 6, 8], F32)
    wq8 = PERS.alloc("wq8", [128, 2], F32)
    g1b = PERS.alloc("g1b", [128, 1024], F32)
    g2b = PERS.alloc("g2b", [128, 1024], F32)
    wkp = PERS.alloc("wkp", [128, 2, NT, 4], F32)
    eden = PERS.alloc("eden", [128, 2, NT, 4], F32)
    rbt = PERS.alloc("rbt", [128, 2, NT, 4], F32)
    vatt = PERS.alloc("vatt", [128, NT, 256], BF16)
    sexp = PERS.alloc("sexp", [128, 16], F32)
    S.dma("sp", w=[("cstf",)], out=cstf[:], in_=cf_d[:, 0:C_SWF])
    S.dma("sp", w=[("vecs",)], out=vecs[:], in_=vecs_d)
    S.dma("pool", w=[("cstb",)], out=cstb[:], in_=cb_d)
    identf = cstf[:, C_IDF:C_IDF + 128]
    trif = cstf[:, C_TRF:C_TRF + 128]
    trib = cstf[:, C_TRB:C_TRB + 128]
    onesf = cstf[:, C_ONF:C_ONF + 128]
    permf = cstf[:, C_PMF:C_PMF + 128]
    identb = cstb[:, B_IDB:B_IDB + 128]
    bdiag = cstb[:, B_BD:B_BD + 128]
    permb = cstb[:, B_PM:B_PM + 128]
    CF, CB, VK, GK = ("cstf",), ("cstb",), ("vecs",), ("gates",)

    S.ph = 1
    scb = PC.alloc("scb", [128, 8, 2], BF16)
    screp = PC.alloc("screp", [128, 8, 128], BF16)
    bmrow = PC.alloc("bmrow", [1, 4, 512], F32)
    wmod = [PC.alloc(f"wmod{i}", [128, 8, 512], BF16) for i in range(2)]
    for i_, blk_ in enumerate((4, 5, 10, 11)):
        S.dma("sp", w=[("bmrow", i_)], out=bmrow[0:1, i_, :], in_=bmod_d[0:1, blk_ * 512:(blk_ + 1) * 512])
    S.op("act", "activation", r=[VK], w=[("scb",)], out=scb[:, :, 0], in_=vecs[:, V_CC:V_CC + 8], func=AF.Silu)
    S.op("act", "activation", r=[VK], w=[("scb",)], out=scb[:, :, 1], in_=vecs[:, V_CCX:V_CCX + 8], func=AF.Silu)
    S.op("dve", "tensor_copy", r=[("scb",)], w=[("screp",)], out=screp[:],
         in_=scb[:, :, 0:1].to_broadcast([128, 8, 128]))
    psm = psb[6][:, 0:96].rearrange("p (a b) -> p a b", b=2)
    def mk_aff(ia, ib, sc_ch, sh_ch, which, nw_col):
        mk = [("modT", sc_ch // 16), ("modT", sh_ch // 16)]
        S.op("dve", "scalar_tensor_tensor", r=mk + [VK], w=[("aff", ia)], out=aff[:, ia, :],
             in0=modT[:, sc_ch:sc_ch + 8, which], scalar=1.0, in1=vecs[:, nw_col:nw_col + 8], op0=ADD, op1=MUL)
        S.op("dve", "tensor_copy", r=mk, w=[("aff", ib)], out=aff[:, ib, :], in_=modT[:, sh_ch:sh_ch + 8, which])
    def wmod_block(blk, gate=()):
            wt = wmod[blk % 2]
            wk = (f"wmod{blk % 2}",)
            S.dma("pool", r=list(gate), w=[wk], out=wt[:], in_=wmod_d.rearrange("(kc p) n -> p kc n", p=128)[:, :, blk * 512:(blk + 1) * 512], max_dma_last_dim=8192)
            for j in range(4):
                ch = blk * 4 + j
                S.mm([("matmul", dict(out=psm[:, ch, :], lhsT=wt[:, kc, j * 128:(j + 1) * 128], rhs=scb[:, kc, :],
                                      start=(kc == 0), stop=(kc == 7))) for kc in range(8)],
                     r=[wk, ("scb",)], w=[PSK(6)])
            if blk % 4 == 3:
                c0_ = (blk // 4) * 16
                S.op("dve", "tensor_tensor", r=[PSK(6), VK], w=[("modT", blk // 4)], out=modT[:, c0_:c0_ + 16, :], in0=psm[:, c0_:c0_ + 16, :],
                     in1=vecs[:, V_BMOD + c0_:V_BMOD + c0_ + 16][:, :, None].to_broadcast([128, 16, 2]), op=ADD)
            if blk == 3:
                mk_aff(0, 1, 8, 0, 0, V_N1W)
                mk_aff(2, 3, 8, 0, 1, V_N1W)
            if blk == 11:
                mk_aff(4, 5, 32, 24, 0, V_N2W)
            if blk in (4, 5, 10, 11):
                bank = 7
                ins = [("matmul", dict(out=psb[bank][:], lhsT=screp[:, kc, :], rhs=wt[:, kc, :],
                                       start=(kc == 0), stop=False)) for kc in range(8)]
                bi_ = (4, 5, 10, 11).index(blk)
                ins.append(("matmul", dict(out=psb[bank][:], lhsT=onesf[0:1, :], rhs=bmrow[0:1, bi_, :],
                                           start=False, stop=True)))
                S.mm(ins, r=[wk, ("screp",), ("bmrow", bi_), CF], w=[PSK(bank)])
                dst = (g1b if blk < 6 else g2b)[:, (blk % 2) * 512:(blk % 2 + 1) * 512]
                S.op("dve", "tensor_copy", r=[PSK(bank)], w=[("g1b",) if blk < 6 else ("g2b",)], out=dst, in_=psb[bank][:])


    for blk in range(4):
        wmod_block(blk)
    S.op("act", "mul", r=[VK], w=[("wq8",)], out=wq8[:, 0:1], in_=vecs[:, V_QNW:V_QNW + 1], mul=0.125)
    S.op("act", "copy", r=[VK], w=[("wq8",)], out=wq8[:, 1:2], in_=vecs[:, V_KNW:V_KNW + 1])
    S.op("act", "activation", r=[VK], w=[("sexp",)], out=sexp[:], in_=vecs[:, V_SINK:V_SINK + 16], func=AF.Exp)

    S.ph = 2
    hT = PA.alloc("hT", [128, 8, TOK], BF16)

    def alloc_norm_bufs(P, nbuf):
        xin = [P.alloc(f"xin{i}", [128, 1024], F32) for i in range(nbuf)]
        xn = [P.alloc(f"xn{i}", [128, 1024], F32) for i in range(nbuf)]
        junk = P.alloc("junk", [128, 1024], BF16)
        st4 = P.alloc("st4", [128, nbuf, 4], F32)
        return xin, xn, junk, st4

    def norm_tile(junk, src_key, src_ap, xn_t, xn_key, stv, st_key, ndim):
        S.op("act", "activation", r=[src_key], w=[("junk",), st_key], out=junk[:, 0:ndim], in_=src_ap, func=AF.Square,
             accum_out=stv[:, 0:1])
        S.op("act", "activation", r=[st_key], w=[st_key], out=stv[:, 1:2], in_=stv[:, 0:1], func=AF.Ln,
             scale=1.0 / ndim, bias=EPS)
        S.op("act", "activation", r=[st_key], w=[st_key], out=stv[:, 2:3], in_=stv[:, 1:2], func=AF.Exp, scale=-0.5)
        S.op("dve", "tensor_scalar", r=[src_key, st_key], w=[xn_key], out=xn_t, in0=src_ap, scalar1=stv[:, 2:3],
             scalar2=None, op0=MUL)

    def to_featmajor(xn_t, xn_key, dstT, dst_key, col0, ia, ib, banks):
        for half in range(2):
            bk = banks[half]
            S.mm([("transpose", dict(out=psb[bk][:, j * 128:(j + 1) * 128], in_=xn_t[:, (half * 4 + j) * 128:(half * 4 + j + 1) * 128],
                                     identity=identf)) for j in range(4)], r=[xn_key, CF], w=[PSK(bk)])
            for j in range(4):
                kc = half * 4 + j
                S.op("act", "activation", r=[PSK(bk), ("aff", ia), ("aff", ib)], w=[dst_key], out=dstT[:, kc, col0:col0 + 128],
                     in_=psb[bk][:, j * 128:(j + 1) * 128], func=AF.Identity, scale=aff[:, ia, kc:kc + 1], bias=aff[:, ib, kc:kc + 1])

    xin, xn, junk, st4 = alloc_norm_bufs(PC, 3)
    for t in range(NT):
        bi = t % 3
        src = ctx_d[t * 128:(t + 1) * 128, :] if t < 2 else x_d[(t - 2) * 128:(t - 1) * 128, :]
        S.dma("sp", w=[(f"xin{bi}",)], out=xin[bi][:], in_=src)
        norm_tile(junk, (f"xin{bi}",), xin[bi][:], xn[bi][:], (f"xn{bi}",), st4[:, bi, :], ("st4", bi), 1024)
        ia, ib = (2, 3) if t < 2 else (0, 1)
        to_featmajor(xn[bi], (f"xn{bi}",), hT, ("hT", t), t * 128, ia, ib, (2 * bi, 2 * bi + 1))
    for blk in range(4, 12):
        wmod_block(blk, gate=[("hT", min(NT - 1, 2 * (blk - 3) + 1))])
    dump("modT", modT[:], [128, 48, 2], [("modT", i) for i in range(3)])
    dump("g1b", g1b[:], [128, 1024], [("g1b",)])
    dump("hT", hT[:], [128, 8, TOK], [("hT", t) for t in range(NT)])
    if stop_after <= 2:
        return finish()

    S.ph = 3
    w3g = PC.alloc("w3g", [128, 8, 272], BF16)
    gpre = PC.alloc("gpre", [128, NT, 16], F32)
    wload(w3g[:, :, 0:16], win_d, O_MG, 16, ("w3g",))
    wload(w3g[:, :, 16:272], win_d, O_AV, 256, ("w3g",))
    for t in range(NT):
        bk = t % 2
        S.mm([("matmul", dict(out=psb[bk][:, 0:272], lhsT=hT[:, kc, t * 128:(t + 1) * 128], rhs=w3g[:, kc, :],
                              start=(kc == 0), stop=(kc == 7))) for kc in range(8)], r=[("hT", t), ("w3g",)], w=[PSK(bk)])
        S.op("act", "copy", r=[PSK(bk)], w=[("gpre",)], out=gpre[:, t, :], in_=psb[bk][:, 0:16])
        S.op("dve", "tensor_copy", r=[PSK(bk)], w=[("vatt",)], out=vatt[:, t, :], in_=psb[bk][:, 16:272])

    S.ph = 4
    GI = PC.alloc("GI", [128, 2, NT, 4], F32)
    GF = PC.alloc("GF", [128, 2, NT, 4], F32)
    SPt = PC.alloc("SPt", [128, 2, NT, 4], F32)
    BN = PC.alloc("BN", [128, 2, NT, 4], F32)
    Gt = PC.alloc("Gt", [128, 2, NT, 4], F32)
    D1 = PC.alloc("D1", [128, 2, NT, 4], F32)
    D2 = PC.alloc("D2", [128, 2, NT, 4], F32)
    gmx = PC.alloc("gmx", [128, 2], F32)
    ROW = PC.alloc("ROW", [1, 288], F32)
    GR = PC.alloc("GR", [1, 288], F32)
    Mst = PC.alloc("Mst", [1, 2, 4], F32)
    gp5 = gpre[:].rearrange("p c (d g h) -> p c d g h", d=2, g=2)
    gbv = vecs[:, V_GB:V_GB + 16].rearrange("p (d g h) -> p d g h", d=2, g=2)
    for d in range(2):
        S.op("dve", "tensor_tensor", r=[("gpre",), VK], w=[("GI",)], out=GI[:, d], in0=gp5[:, :, d, 0, :],
             in1=gbv[:, d, 0, :][:, None, :].to_broadcast([128, NT, 4]), op=ADD)
        S.op("dve", "tensor_tensor", r=[("gpre",), VK], w=[("GF",)], out=GF[:, d], in0=gp5[:, :, d, 1, :],
             in1=gbv[:, d, 1, :][:, None, :].to_broadcast([128, NT, 4]), op=ADD)
    fl = lambda t: t[:].rearrange("p d c h -> p (d c h)")
    S.op("act", "activation", r=[("GF",)], w=[("SPt",)], out=fl(SPt), in_=fl(GF), func=AF.Exp, scale=-1.0)
    S.op("act", "activation", r=[("SPt",)], w=[("SPt",)], out=fl(SPt), in_=fl(SPt), func=AF.Ln, bias=1.0)
    S.mm([("matmul", dict(out=psb[0][:, 0:72], lhsT=trif, rhs=fl(SPt)[:, 0:72], start=True, stop=True)),
          ("matmul", dict(out=psb[0][:, 72:144], lhsT=trib, rhs=fl(SPt)[:, 72:144], start=True, stop=True)),
          ("matmul", dict(out=psb[1][:, 0:144], lhsT=onesf, rhs=fl(SPt), start=True, stop=True))],
         r=[("SPt",), CF], w=[PSK(0), PSK(1)])
    S.op("dve", "tensor_copy", r=[PSK(0)], w=[("BN",)], out=fl(BN), in_=psb[0][:, 0:144])
    S.op("dve", "tensor_tensor", r=[("BN",), ("GI",)], w=[("Gt",)], out=fl(Gt), in0=fl(GI), in1=fl(BN), op=ADD)
    S.op("dve", "tensor_copy", r=[PSK(1)], w=[("ROW",)], out=ROW[0:1, 144:288], in_=psb[1][0:1, 0:144])
    S.mm([("transpose", dict(out=psb[2][0:72, 0:128], in_=fl(Gt)[:, 0:72], identity=identf)),
          ("transpose", dict(out=psb[2][0:72, 128:256], in_=fl(Gt)[:, 72:144], identity=identf))],
         r=[("Gt",), CF], w=[PSK(2)])
    for d in range(2):
        S.op("dve", "tensor_reduce", r=[PSK(2)], w=[("gmx",)], out=gmx[0:72, d:d + 1], in_=psb[2][0:72, d * 128:(d + 1) * 128],
             axis=AX.X, op=MAX)
    S.mm([("transpose", dict(out=psb[3][0:1, 0:72], in_=gmx[0:72, 0:1], identity=identf[0:72, 0:72])),
          ("transpose", dict(out=psb[3][0:1, 72:144], in_=gmx[0:72, 1:2], identity=identf[0:72, 0:72]))],
         r=[("gmx",), CF], w=[PSK(3)])
    S.op("dve", "tensor_copy", r=[PSK(3)], w=[("ROW",)], out=ROW[0:1, 0:144], in_=psb[3][0:1, 0:144])
    S.op("dve", "memset", w=[("Mst", 0), ("Mst", 1)], ap=Mst[0:1], constant=0.0)
    order = [list(range(NT)), [1, 0] + list(range(NT - 1, 1, -1))]
    for k in range(NT):
        for d in range(2):
            c = order[d][k]
            i0 = d * 72 + c * 4
            gsl = GR[0:1, i0:i0 + 4]
            S.op("dve", "tensor_tensor", r=[("Mst", d), ("ROW",)], w=[("GR",)], out=gsl, in0=Mst[0:1, d, :], in1=ROW[0:1, i0:i0 + 4], op=MAX)
            S.op("dve", "tensor_tensor", r=[("Mst", d), ("GR",)], w=[("GR",)], out=GR[0:1, 144 + i0:144 + i0 + 4], in0=Mst[0:1, d, :], in1=gsl, op=SUB)
            S.op("dve", "tensor_tensor", r=[("GR",), ("ROW",)], w=[("Mst", d)], out=Mst[0:1, d, :], in0=gsl, in1=ROW[0:1, 144 + i0:144 + i0 + 4], op=SUB)
    S.mm([("matmul", dict(out=psb[4][:, 0:288], lhsT=onesf[0:1, :], rhs=GR[0:1, :], start=True, stop=True))],
         r=[("GR",), CF], w=[PSK(4)])
    S.op("dve", "tensor_tensor", r=[("Gt",), PSK(4)], w=[("D1",)], out=fl(D1), in0=fl(Gt), in1=psb[4][:, 0:144], op=SUB)
    S.op("act", "activation", r=[("D1",)], w=[GK], out=fl(wkp), in_=fl(D1), func=AF.Exp, bias=float(-0.5 * np.log(128.0)))
    S.op("dve", "tensor_tensor", r=[("BN",), PSK(4)], w=[("D2",)], out=fl(D2), in0=fl(BN), in1=psb[4][:, 0:144], op=SUB)
    S.op("act", "activation", r=[("D2",)], w=[GK], out=fl(eden), in_=fl(D2), func=AF.Exp)
    S.op("act", "activation", r=[PSK(4)], w=[GK], out=fl(rbt), in_=psb[4][:, 144:288], func=AF.Exp)
    dump("wkp", wkp[:], [128, 2, NT, 4], [GK])
    dump("eden", eden[:], [128, 2, NT, 4], [GK])
    dump("rbt", rbt[:], [128, 2, NT, 4], [GK])
    dump("vatt", vatt[:], [128, NT, 256], [("vatt",)])
    if stop_after <= 4:
        return finish()

    S.ph = 5
    PC.reset()
    PW.reset()
    mlT = PC.alloc("mlT", [128, 8, L], BF16)
    attT = PC.alloc("attT", [128, 8, L], BF16)
    wq_hs = [PW.alloc(f"wq_h{i}", [128, 8, 128], BF16) for i in range(2)]
    wk_hs = [PW.alloc(f"wk_h{i}", [128, 8, 128], BF16) for i in range(2)]
    wv_h = PW.alloc("wv_h", [128, 8, 256], BF16)
    wo_h = PW.alloc("wo_h", [128, 8, 256], BF16)
    qTs = [PW.alloc(f"qT{i}", [128, L], BF16) for i in range(2)]
    kTs = [PW.alloc(f"kT{i}", [128, L], BF16) for i in range(2)]
    osig = PW.alloc("osig", [128, 2, L], BF16)
    ktoks = [PW.alloc(f"ktok{i}", [128, NT, 128], BF16) for i in range(2)]
    vexts = [PW.alloc(f"vext{i}", [128, NT, 260], BF16) for i in range(2)]
    hbuf = PW.alloc("hbuf", [128, 16, 256], BF16)
    C32 = PW.alloc("C32", [128, 2, 260], F32)
    Cs = PW.alloc("Cs", [128, 2, 260], BF16)
    Sm = PW.alloc("Sm", [128, 2, 2, 128], BF16)
    kw = PW.alloc("kw", [128, 2, 2, 128], BF16)
    hn = PW.alloc("hn", [128, 256], BF16)
    den = PW.alloc("den", [128, 2, 2], F32)
    fst = PW.alloc("fst", [128, 4], F32)
    junk5 = PW.alloc("junk5", [128, 256], BF16)
    maskap = [cstb[:, B_MLE:B_MLE + 128], cstb[:, B_MGE:B_MGE + 128]]
    for i_ in range(2):
        S.op("dve", "memset", w=[(f"vext{i_}", "ones")], ap=vexts[i_][:, :, 256:260], constant=0.0)
        S.op("dve", "memset", r=[], w=[(f"vext{i_}", "ones")], ap=vexts[i_][:, :, 256:257], constant=1.0)
    if stop_after <= 4.05:
        return finish()
    for hd in range(4):
        hb_ = hd % 2
        wq_h, wk_h, qT, kT, ktok = wq_hs[hb_], wk_hs[hb_], qTs[hb_], kTs[hb_], ktoks[hb_]
        QT, KT, KTOK, WQ, WK = f"qT{hb_}", f"kT{hb_}", f"ktok{hb_}", (f"wq_h{hb_}",), (f"wk_h{hb_}",)
        vext, VX = vexts[hb_], f"vext{hb_}"
        wload(wq_h[:], win_d, O_MQ + hd * 128, 128, WQ)
        wload(wk_h[:], win_d, O_MK + hd * 128, 128, WK)
        wload(wv_h[:], win_d, O_MV + hd * 256, 256, ("wv_h",))
        wload(wo_h[:], win_d, O_MO + hd * 256, 256, ("wo_h",))
        nb = 0
        for tg in range(4):
            tok0 = 256 + tg * 512
            rk = [("hT", 2 + tg * 4 + i) for i in range(4)]

            def fm(wt, wkey, cols):
                nonlocal nb
                bk = FMB + (nb % 2)
                nb += 1
                S.mm([("matmul", dict(out=psb[bk][:], lhsT=wt[:, kc, cols], rhs=hT[:, kc, tok0:tok0 + 512],
                                      start=(kc == 0), stop=(kc == 7))) for kc in range(8)], r=[wkey] + rk, w=[PSK(bk)])
                return bk
            bk = fm(wq_h, WQ, slice(0, 128))
            S.op("act", "copy", r=[PSK(bk)], w=[(QT, tg)], out=qT[:, tg * 512:(tg + 1) * 512], in_=psb[bk][:])
            if stop_after <= 4.1:
                return finish()
            bk = fm(wk_h, WK, slice(0, 128))
            S.op("dve", "tensor_copy", r=[PSK(bk)], w=[(KT, tg)], out=kT[:, tg * 512:(tg + 1) * 512], in_=psb[bk][:])
        if stop_after <= 4.15:
            return finish()
        for t in range(NT):
            bk = FMB + (t % 2)
            kpart = [("matmul", dict(out=psb[bk][:, 0:128], lhsT=hT[:, kc, t * 128:(t + 1) * 128], rhs=wk_h[:, kc, :],
                                     start=(kc == 0), stop=(kc == 7))) for kc in range(8)] if t < 2 else []
            S.mm(kpart +
                 [("matmul", dict(out=psb[bk][:, 128:384], lhsT=hT[:, kc, t * 128:(t + 1) * 128], rhs=wv_h[:, kc, :],
                                  start=(kc == 0), stop=(kc == 7))) for kc in range(8)],
                 r=[("hT", t), WK, ("wv_h",)], w=[PSK(bk)])
            if stop_after <= 4.16:
                return finish()
            if t < 2:
                S.op("dve", "tensor_copy", r=[PSK(bk)], w=[(KTOK, t)], out=ktok[:, t, :], in_=psb[bk][:, 0:128])
            if stop_after <= 4.17:
                return finish()
            S.op("act", "copy", r=[PSK(bk)], w=[(VX, t)], out=vext[:, t, 0:256], in_=psb[bk][:, 128:384])
            if stop_after <= 4.18:
                return finish()
        if stop_after <= 4.19:
            return finish()
        for tg in range(4):
            bk = FMB + (tg % 2)
            ptb = psb[bk][:].bitcast(BF16)
            S.mm([("transpose", dict(out=ptb[:, j * 128:(j + 1) * 128], in_=kT[:, (tg * 4 + j) * 128:(tg * 4 + j + 1) * 128], identity=identb))
                  for j in range(4)], r=[(KT, tg), CB], w=[PSK(bk)])
            S.op("dve", "tensor_copy", r=[PSK(bk)], w=[(KTOK, 2 + tg * 4 + j) for j in range(4)], out=ktok[:, 2 + tg * 4:2 + tg * 4 + 4, :],
                 in_=ptb[:, 0:512].rearrange("p (j d) -> p j d", j=4))
        for tg in range(4):
            tok0 = 256 + tg * 512
            rk = [("hT", 2 + tg * 4 + i) for i in range(4)]
            for e in range(2):
                bk = fm(wo_h, ("wo_h",), slice(e * 128, (e + 1) * 128))
                S.op("act", "activation", r=[PSK(bk)], w=[("osig", tg)], out=osig[:, e, tg * 512:(tg + 1) * 512], in_=psb[bk][:],
                     func=AF.Sigmoid)
        S.op("dve", "memset", w=[("C32", 0), ("C32", 1)], ap=C32[:], constant=0.0)
        if stop_after <= 4.2:
            return finish()

        def finalize(lc, d):
            pstb = psb[2 + d][:].bitcast(BF16)[:, 768:1024]
            S.op("act", "activation", r=[("hbuf", lc)], w=[("junk5",), ("fst",)], out=junk5[:], in_=hbuf[:, lc, :], func=AF.Square,
                 accum_out=fst[:, 0:1])
            S.op("act", "activation", r=[("fst",)], w=[("fst",)], out=fst[:, 1:2], in_=fst[:, 0:1], func=AF.Ln, scale=1.0 / 256, bias=EPS)
            S.op("act", "activation", r=[("fst",)], w=[("fst",)], out=fst[:, 2:3], in_=fst[:, 1:2], func=AF.Exp, scale=-0.5)
            S.op("pool", "tensor_scalar", r=[("hbuf", lc), ("fst",)], w=[("hn",)], out=hn[:], in0=hbuf[:, lc, :], scalar1=fst[:, 2:3],
                 scalar2=1.0, op0=MUL, op1=MUL)
            S.mm([("transpose", dict(out=pstb[:, e * 128:(e + 1) * 128], in_=hn[:, e * 128:(e + 1) * 128], identity=identb))
                  for e in range(2)], r=[("hn",), CB], w=[PSK(2 + d)])
            for e in range(2):
                ch = hd * 2 + e
                S.op("dve", "scalar_tensor_tensor", r=[PSK(2 + d), VK, ("osig", lc // 4)], w=[("mlT", ch, lc)],
                     out=mlT[:, ch, lc * 128:(lc + 1) * 128], in0=pstb[:, e * 128:(e + 1) * 128],
                     scalar=vecs[:, V_MLNW + ch:V_MLNW + ch + 1], in1=osig[:, e, lc * 128:(lc + 1) * 128], op0=MUL, op1=MUL)

        def stage_a(k, d):
            c = order[d][k]
            p = k % 2
            lc = c - 2
            wkc = wkp[:, d, c, hd:hd + 1]
            if c >= 2:
                tsl = slice(lc * 128, (lc + 1) * 128)
                bk = d
                S.mm([("matmul", dict(out=psb[bk][:, 0:128], lhsT=kT[:, tsl], rhs=qT[:, tsl], start=True, stop=True))],
                     r=[(KT, lc // 4), (QT, lc // 4)], w=[PSK(bk)])
                S.op("dve", "scalar_tensor_tensor", r=[PSK(bk), GK, CB], w=[("Sm", d, p)], out=Sm[:, d, p, :], in0=psb[bk][:, 0:128],
                     scalar=wkc, in1=maskap[d], op0=MUL, op1=MUL)
            S.op("pool", "tensor_scalar", r=[(KTOK, c), GK], w=[("kw", d, p)], out=kw[:, d, p, :], in0=ktok[:, c, :], scalar1=wkc,
                 scalar2=1.0, op0=MUL, op1=MUL)

        def stage_b(k, d):
            c = order[d][k]
            p = k % 2
            lat = c >= 2
            lc = c - 2
            rc = rbt[:, d, c, hd:hd + 1]
            ed = eden[:, d, c, hd:hd + 1]
            tsl = slice(lc * 128, (lc + 1) * 128)
            if lat:
                S.op("act", "activation", r=[("C32", d), GK], w=[("Cs", d)], out=Cs[:, d, 0:258], in_=C32[:, d, 0:258],
                     func=AF.Copy, scale=rc)
                S.mm([("matmul", dict(out=psb[2 + d][:, 0:258], lhsT=Sm[:, d, p, :], rhs=vext[:, c, 0:258], start=True, stop=False)),
                      ("matmul", dict(out=psb[2 + d][:, 0:258], lhsT=qT[:, tsl], rhs=Cs[:, d, 0:258], start=False, stop=True))],
                     r=[("Sm", d, p), (VX, c), (VX, "ones"), (QT, lc // 4), ("Cs", d)], w=[PSK(2 + d)])
            S.mm([("matmul", dict(out=psb[4 + d][:, 0:258], lhsT=kw[:, d, p, :], rhs=vext[:, c, 0:258], start=True, stop=True))],
                 r=[("kw", d, p), (VX, c), (VX, "ones")], w=[PSK(4 + d)])
            S.op("dve", "scalar_tensor_tensor", r=[("C32", d), PSK(4 + d), GK], w=[("C32", d)], out=C32[:, d, 0:258],
                 in0=C32[:, d, 0:258], scalar=rc, in1=psb[4 + d][:, 0:258], op0=MUL, op1=ADD)
            if lat:
                S.op("dve", "tensor_scalar", r=[PSK(2 + d), GK], w=[("den", d)], out=den[:, d, 0:1], in0=psb[2 + d][:, 256:257],
                     scalar1=-1.0, scalar2=ed, op0=MUL, op1=MAX)
                S.op("dve", "tensor_tensor", r=[PSK(2 + d), ("den", d)], w=[("den", d)], out=den[:, d, 0:1], in0=psb[2 + d][:, 256:257],
                     in1=den[:, d, 0:1], op=MAX)
                S.op("dve", "reciprocal", r=[("den", d)], w=[("den", d)], out=den[:, d, 1:2], in_=den[:, d, 0:1])
                first = (d == 0 and lc < 8) or (d == 1 and lc >= 8)
                if first:
                    S.op("act", "activation", r=[PSK(2 + d), ("den", d)], w=[("hbuf", lc)], out=hbuf[:, lc, :],
                         in_=psb[2 + d][:, 0:256], func=AF.Copy, scale=den[:, d, 1:2])
                else:
                    S.op("dve", "scalar_tensor_tensor", r=[PSK(2 + d), ("den", d), ("hbuf", lc)], w=[("hbuf", lc)],
                         out=hbuf[:, lc, :], in0=psb[2 + d][:, 0:256], scalar=den[:, d, 1:2], in1=hbuf[:, lc, :], op0=MUL, op1=ADD)
                    finalize(lc, d)

        for k in range(NT + 1):
            for d in range(2):
                if k < NT:
                    stage_a(k, d)
            for d in range(2):
                if k >= 1:
                    stage_b(k - 1, d)
    dump("mlT", mlT[:], [128, 8, L], [("mlT", ch, lc) for ch in range(8) for lc in range(16)])
    if stop_after <= 5:
        return finish()

    S.ph = 6
    PW.reset()
    wq_a = PW.alloc("wq_a", [128, 8, 256], BF16)
    wk_a = PW.alloc("wk_a", [128, 8, 128], BF16)
    cs = [PW.alloc(f"cs{i}", [128, 2, 512], F32) for i in range(2)]
    qTa = PW.alloc("qTa", [128, 2, L], BF16)
    kTlo = PW.alloc("kTlo", [128, TOK], BF16)
    kThi = PW.alloc("kThi", [128, TOK], BF16)
    vA = PW.alloc("vA", [128, NT, 128], BF16)
    vB = PW.alloc("vB", [128, NT, 128], BF16)
    sqs = [PW.alloc(f"sq{i}", [128, 512], BF16) for i in range(2)]
    rss = [PW.alloc(f"rs{i}", [128, 512], F32) for i in range(2)]
    qhs = [PW.alloc(f"qh{i}", [128, 512], F32) for i in range(2)]
    t2s = [PW.alloc(f"t2{i}", [128, 512], F32) for i in range(2)]
    Pt = PW.alloc("Pt", [128, 2, 5, 512], BF16)
    dsum = PW.alloc("dsum", [128, 2, 2, 512], F32)
    swf_t = PW.alloc("swf", [128, 128], BF16)
    drec = PW.alloc("drec", [128, 2, 2, 512], BF16)
    mneg = PW.alloc("mneg", [128, NCB2], BF16)
    S.dma("pool", w=[("mneg",)], out=mneg[:], in_=env["cb2_d"])
    S.dma("pool", w=[("swf",)], out=swf_t[:], in_=cf_d[:, C_SWF:C_SWF + 128])
    swapf = swf_t[:]
    S.op("dve", "memset", w=[("kTlo", "z")], ap=kTlo[64:128, :], constant=0.0)
    S.op("dve", "memset", w=[("kThi", "z")], ap=kThi[0:64, :], constant=0.0)
    S.op("dve", "memset", w=[("vA", "ones")], ap=vA[:, :, 64:128], constant=1.0)
    S.op("dve", "memset", w=[("vB", "ones")], ap=vB[:, :, 0:64], constant=1.0)
    pipe_n = [0]

    def qk_pipeline(wt, wkey, cols, tok0, n, wcol, rope, csb, dsts, dkeys):
        pi_ = pipe_n[0] % 2
        b0 = 3 * pi_
        pipe_n[0] += 1
        sq, rs, qh, t2 = sqs[pi_], rss[pi_], qhs[pi_], t2s[pi_]
        SQ, RS, QH, T2 = (f"sq{pi_}",), (f"rs{pi_}",), (f"qh{pi_}",), (f"t2{pi_}",)
        pq, pms, prot = b0, b0 + 1, b0 + 2
        rk = [("hT", t) for t in range(tok0 // 128, (tok0 + n) // 128)]
        S.mm([("matmul", dict(out=psb[pq][:, 0:n], lhsT=wt[:, kc, cols], rhs=hT[:, kc, tok0:tok0 + n],
                              start=(kc == 0), stop=(kc == 7))) for kc in range(8)], r=[wkey] + rk, w=[PSK(pq)])
        S.op("act", "activation", r=[PSK(pq)], w=[SQ], out=sq[:, 0:n], in_=psb[pq][:, 0:n], func=AF.Square)
        S.mm([("matmul", dict(out=psb[pms][:, 0:n], lhsT=bdiag, rhs=sq[:, 0:n], start=True, stop=True))], r=[SQ, CB], w=[PSK(pms)])
        S.op("act", "activation", r=[PSK(pms)], w=[RS], out=rs[:, 0:n], in_=psb[pms][:, 0:n], func=AF.Ln, bias=EPS)
        S.op("act", "activation", r=[RS], w=[RS], out=rs[:, 0:n], in_=rs[:, 0:n], func=AF.Exp, scale=-0.5)
        S.op("dve", "scalar_tensor_tensor", r=[PSK(pq), ("wq8",), RS], w=[QH], out=qh[:, 0:n], in0=psb[pq][:, 0:n],
             scalar=wcol, in1=rs[:, 0:n], op0=MUL, op1=MUL)
        if rope:
            S.op("pool", "tensor_copy", r=[QH], w=[SQ], out=sq[:, 0:n], in_=qh[:, 0:n])
            S.mm([("matmul", dict(out=psb[prot][:, 0:n], lhsT=permb, rhs=sq[:, 0:n], start=True, stop=True))], r=[SQ, CB], w=[PSK(prot)])
            S.op("dve", "tensor_tensor", r=[PSK(prot), csb[1]], w=[T2], out=t2[:, 0:n], in0=psb[prot][:, 0:n], in1=csb[0][:, 1, 0:n], op=MUL)
            S.op(COS_ENG, "tensor_tensor", r=[QH, csb[1]], w=[QH], out=qh[:, 0:n], in0=qh[:, 0:n], in1=csb[0][:, 0, 0:n], op=MUL)
            for (dst, p0, p1), dk in zip(dsts, dkeys):
                S.op("dve", "tensor_tensor", r=[QH, T2], w=[dk], out=dst, in0=qh[p0:p1, 0:n], in1=t2[p0:p1, 0:n], op=ADD)
        else:
            for (dst, p0, p1), dk in zip(dsts, dkeys):
                S.op("dve", "tensor_copy", r=[QH], w=[dk], out=dst, in_=qh[p0:p1, 0:n])

    for kv in range(4):
        wload(wq_a[:], win_d, O_AQ + kv * 256, 256, ("wq_a",))
        wload(wk_a[:, :, 0:64], win_d, O_AK + kv * 64, 64, ("wk_a",))
        wload(wk_a[:, :, 64:128], win_d, O_AK + kv * 64, 64, ("wk_a",))
        S.op("dve", "tensor_copy", r=[("vatt",)], w=[("vA",)], out=vA[:, :, 0:64], in_=vatt[:, :, kv * 64:(kv + 1) * 64])
        S.op("dve", "tensor_copy", r=[("vatt",)], w=[("vB",)], out=vB[:, :, 64:128], in_=vatt[:, :, kv * 64:(kv + 1) * 64])
        qk_pipeline(wk_a, ("wk_a",), slice(0, 128), 0, 256, wq8[:, 1:2], False, None,
                    [(kTlo[0:64, 0:256], 0, 64), (kThi[64:128, 0:256], 64, 128)], [("kTlo", 0), ("kThi", 0)])
        for tg in range(4):
            cb_ = cs[tg % 2]
            ck = (f"cs{tg % 2}",)
            S.dma("sp", w=[ck], out=cb_[:, 0, :], in_=cf_d[:, C_COS + tg * 512:C_COS + (tg + 1) * 512])
            S.dma("sp", w=[ck], out=cb_[:, 1, :], in_=cf_d[:, C_SIN + tg * 512:C_SIN + (tg + 1) * 512])
            t0 = 256 + tg * 512
            qk_pipeline(wk_a, ("wk_a",), slice(0, 128), t0, 512, wq8[:, 1:2], True, (cb_, ck),
                        [(kTlo[0:64, t0:t0 + 512], 0, 64), (kThi[64:128, t0:t0 + 512], 64, 128)], [("kTlo", 1 + tg), ("kThi", 1 + tg)])
            for e in range(2):
                qk_pipeline(wq_a, ("wq_a",), slice(e * 128, (e + 1) * 128), t0, 512, wq8[:, 0:1], True, (cb_, ck),
                            [(qTa[:, e, tg * 512:(tg + 1) * 512], 0, 128)], [("qTa", e, tg)])
        kall = [("kTlo", i) for i in range(5)] + [("kThi", i) for i in range(5)] + [("kTlo", "z"), ("kThi", "z")]
        sbanks = [0, 1, 6]
        scnt = [0]

        def att_front(qb):
            kts = []
            if qb > 0:
                kts.append((256 + (qb - 1) * 128, 2 + qb - 1, B_MGE))
            kts.append((256 + qb * 128, 2 + qb, None))
            if qb < 15:
                kts.append((256 + (qb + 1) * 128, 2 + qb + 1, B_MLE))
            kts.append((0, 0, None))
            kts.append((128, 1, None))
            u = qb % 2
            for i, (kc0, vt, mask) in enumerate(kts):
                bk = sbanks[scnt[0] % 3]
                scnt[0] += 1
                ins = []
                if mask is not None:
                    mcol = B_NGE if mask == B_MGE else B_NLE
                    ins.append(("matmul", dict(out=psb[bk][:], lhsT=identb, rhs=mneg[:, mcol:mcol + 512], start=True, stop=False)))
                ins += [("matmul", dict(out=psb[bk][:, g * 128:(g + 1) * 128], lhsT=(kTlo if g % 2 == 0 else kThi)[:, kc0:kc0 + 128],
                                        rhs=qTa[:, g // 2, qb * 128:(qb + 1) * 128], start=(mask is None), stop=(mask is None or g == 3)))
                        for g in range(4)]
                S.mm(ins, r=kall + [("qTa", e, qb // 4) for e in range(2)] + [("mneg",), CB], w=[PSK(bk)])
                S.op("act", "activation", r=[PSK(bk)], w=[("Pt", u, i)], out=Pt[:, u, i, :], in_=psb[bk][:], func=AF.Exp)
            return kts

        def att_back(qb, kts):
            nk = len(kts)
            u = qb % 2
            o1b, o2b = (2, 3) if qb % 2 == 0 else (4, 5)
            v4 = lambda ap: ap.rearrange("p (e h t) -> p e h t", e=2, h=2)
            S.mm([("matmul", dict(out=psb[o1b][:, 0:256], lhsT=vA[:, kts[i][1], :], rhs=v4(Pt[:, u, i, :])[:, :, 0, :],
                                  start=(i == 0), stop=(i == nk - 1)))
                  for i in range(nk)], r=[("Pt", u, i) for i in range(nk)] + [("vA",), ("vA", "ones")], w=[PSK(o1b)])
            S.mm([("matmul", dict(out=psb[o2b][:, 0:256], lhsT=vB[:, kts[i][1], :], rhs=v4(Pt[:, u, i, :])[:, :, 1, :],
                                  start=(i == 0), stop=(i == nk - 1)))
                  for i in range(nk)], r=[("Pt", u, i) for i in range(nk)] + [("vB",), ("vB", "ones")], w=[PSK(o2b)])
            v3 = lambda ap: ap.rearrange("p (e t) -> p e t", e=2)
            qsl = slice(qb * 128, (qb + 1) * 128)
            tgb = (qb // 4) % 2
            dsl = slice((qb % 4) * 128, (qb % 4 + 1) * 128)
            dk = ("dsum", tgb, qb % 4)
            S.op("dve", "tensor_copy", r=[PSK(o1b)], w=[("attT", kv, qb)], out=attT[0:64, 2 * kv:2 * kv + 2, qsl], in_=v3(psb[o1b][0:64, 0:256]))
            S.op("dve", "tensor_copy", r=[PSK(o1b)], w=[dk], out=dsum[64:128, tgb, :, dsl], in_=v3(psb[o1b][64:128, 0:256]))
            S.op("dve", "tensor_copy", r=[PSK(o2b)], w=[("attT", kv, qb)], out=attT[64:128, 2 * kv:2 * kv + 2, qsl], in_=v3(psb[o2b][64:128, 0:256]))
            S.op("dve", "tensor_copy", r=[PSK(o2b)], w=[dk], out=dsum[0:64, tgb, :, dsl], in_=v3(psb[o2b][0:64, 0:256]))
            if qb % 4 == 3:
                att_norm(qb // 4)

        def att_norm(tg):
            tgb = tg % 2
            dkeys = [("dsum", tgb, j) for j in range(4)]
            akeys = [("attT", kv, qb) for qb in range(tg * 4, tg * 4 + 4)]
            gsl = slice(tg * 512, (tg + 1) * 512)
            for e in range(2):
                for hf in range(2):
                    hcol = kv * 4 + 2 * e + hf
                    psl = slice((1 - hf) * 64, (2 - hf) * 64)
                    S.op("dve", "tensor_scalar", r=dkeys + [("sexp",)], w=dkeys, out=dsum[psl, tgb, e, :], in0=dsum[psl, tgb, e, :],
                         scalar1=sexp[psl, hcol:hcol + 1], scalar2=None, op0=ADD)
            rkeys = [("drec", tgb)]
            S.op("dve", "reciprocal", r=dkeys, w=dkeys, out=dsum[:, tgb], in_=dsum[:, tgb])
            ni = S.op("act", "copy", r=dkeys, w=rkeys, out=drec[:, tgb], in_=dsum[:, tgb])
            S.nodes[ni]["lag"] = NORM_SLACK
            for e in range(2):
                bk = 7
                S.mm([("matmul", dict(out=psb[bk][:], lhsT=swapf, rhs=drec[:, tgb, e, :], start=True, stop=True))], r=rkeys + [("swf",)], w=[PSK(bk)])
                S.op("dve", "tensor_tensor", r=akeys + [PSK(bk)], w=akeys, out=attT[:, 2 * kv + e, gsl], in0=attT[:, 2 * kv + e, gsl],
                     in1=psb[bk][:], op=MUL)

        prev = None
        for qb in range(16):
            kts = att_front(qb)
            if prev is not None:
                att_back(*prev)
            prev = (qb, kts)
        att_back(*prev)
    dump("attT", attT[:], [128, 8, L], [("attT", kv, qb) for kv in range(4) for qb in range(16)])
    if stop_after <= 6:
        return finish()

    S.ph = 7
    PW.reset()
    P7A = Pool(nc, S, "p7a", o3, o3 + 16384)
    P7B = Pool(nc, S, "p7b", o3 + 16384, SBUF_LIMIT)
    P7A.pending = set(PW.pending)
    P7B.pending = set(PW.pending)
    wch = [P7A.alloc(f"wch{i}", [128, 4, 8, 128], BF16) for i in range(2)]
    wout = P7B.alloc("wout", [128, 8, 1024], BF16)
    ymT = P7B.alloc("ymT", [128, 8, L], BF16)
    sga = [P7B.alloc(f"sga{i}", [128, 512], F32) for i in range(2)]
    sgm = [P7B.alloc(f"sgm{i}", [128, 512], F32) for i in range(2)]
    xt = [P7B.alloc(f"xt{i}", [128, 1024], F32) for i in range(2)]
    wload(wout[:], wout_d, 0, 1024, ("wout",))
    for kc in range(8):
        S.op("pool", "tensor_tensor", r=[("wout",), ("g1b",)], w=[("wout",)], out=wout[:, kc, :], in0=wout[:, kc, :], in1=g1b[:], op=MUL)
    att_keys = [("attT", kv, qb) for kv in range(4) for qb in range(16)]
    ml_keys = [("mlT", ch, lc) for ch in range(8) for lc in range(16)]
    it = 0
    for oc in range(8):
        bi = oc % 2
        w_ = wch[bi]
        wk_ = (f"wch{bi}",)
        wload(w_[:, 0], win_d, O_GA + oc * 128, 128, wk_)
        wload(w_[:, 1], win_d, O_GM + oc * 128, 128, wk_)
        wload(w_[:, 2], wba_d, oc * 128, 128, wk_)
        wload(w_[:, 3], wbm_d, oc * 128, 128, wk_)
        for tg in range(4):
            tok0 = 256 + tg * 512
            rk = [("hT", 2 + tg * 4 + i) for i in range(4)]
            pb0 = 4 * (it % 2)
            sb = it % 2
            it += 1
            srcs = [(hT, tok0, rk), (hT, tok0, rk), (attT, tg * 512, att_keys), (mlT, tg * 512, ml_keys)]
            for j, (src, c0, keys) in enumerate(srcs):
                S.mm([("matmul", dict(out=psb[pb0 + j][:], lhsT=w_[:, j, kc, :], rhs=src[:, kc, c0:c0 + 512],
                                      start=(kc == 0), stop=(kc == 7))) for kc in range(8)], r=[wk_] + keys, w=[PSK(pb0 + j)])
            ka, km = (f"sga{sb}",), (f"sgm{sb}",)
            S.op("act", "activation", r=[PSK(pb0)], w=[ka], out=sga[sb][:], in_=psb[pb0][:], func=AF.Sigmoid)
            S.op("act", "activation", r=[PSK(pb0 + 1)], w=[km], out=sgm[sb][:], in_=psb[pb0 + 1][:], func=AF.Sigmoid)
            S.op("dve", "tensor_tensor", r=[PSK(pb0 + 2), ka], w=[ka], out=sga[sb][:], in0=psb[pb0 + 2][:], in1=sga[sb][:], op=MUL)
            S.op("dve", "tensor_tensor", r=[PSK(pb0 + 3), km], w=[km], out=sgm[sb][:], in0=psb[pb0 + 3][:], in1=sgm[sb][:], op=MUL)
            S.op("pool", "tensor_tensor", r=[ka, km], w=[("ymT", oc, tg)], out=ymT[:, oc, tg * 512:(tg + 1) * 512], in0=sga[sb][:], in1=sgm[sb][:], op=ADD)
    PA.reset()
    P7A.reset()
    h2T = PA.alloc("h2T", [128, 8, L], BF16)
    xn7 = [P7A.alloc(f"xn{i}", [128, 1024], F32) for i in range(2)]
    junk7 = P7A.alloc("junk", [128, 1024], BF16)
    st7 = P7A.alloc("st4", [128, 2, 4], F32)
    for tile in range(16):
        bi = tile % 2
        xk = (f"xt{bi}",)
        S.dma("sp", w=[xk], out=xt[bi][:], in_=x_d[tile * 128:(tile + 1) * 128, :])
        for nb_ in range(2):
            bk = 2 * bi + nb_
            S.mm([("matmul", dict(out=psb[bk][:], lhsT=ymT[:, kc, tile * 128:(tile + 1) * 128], rhs=wout[:, kc, nb_ * 512:(nb_ + 1) * 512],
                                  start=(kc == 0), stop=(kc == 7))) for kc in range(8)], r=[("ymT", oc, tile // 4) for oc in range(8)] + [("wout",)],
                 w=[PSK(bk)])
            S.op("dve", "tensor_tensor", r=[PSK(bk), xk], w=[xk], out=xt[bi][:, nb_ * 512:(nb_ + 1) * 512], in0=psb[bk][:],
                 in1=xt[bi][:, nb_ * 512:(nb_ + 1) * 512], op=ADD)
        S.dma("sp", r=[xk], w=[("xmid", tile)], out=xmid_d[tile * 128:(tile + 1) * 128, :], in_=xt[bi][:])
        norm_tile(junk7, xk, xt[bi][:], xn7[bi][:], (f"xn{bi}",), st7[:, bi, :], ("st4", bi), 1024)
        to_featmajor(xn7[bi], (f"xn{bi}",), h2T, ("h2T", tile), tile * 128, 4, 5, (4 + 2 * bi, 5 + 2 * bi))
    if debug:
        t = dbg_out("xmid", [L, D])
        S.dma("sp", r=[("xmid", i) for i in range(16)], w=[("dbg", "xmid")], out=t, in_=xmid_d)
    if stop_after <= 7:
        return finish()

    S.ph = 8
    PC.reset()
    PW.pending = PW.pending | S.events_of(P7A.names + P7B.names) | P7A.pending | P7B.pending
    PW.reset()
    wua = PC.alloc("wua", [128, 11, 8, 128], BF16)
    wug = PC.alloc("wug", [128, 11, 8, 128], BF16)
    wdn = PW.alloc("wdn", [128, 11, 1024], BF16)
    actT = PW.alloc("actT", [128, 11, 384], BF16)
    accA = [PW.alloc(f"accA{i}", [128, 384], F32) for i in range(3)]
    accG = [PW.alloc(f"accG{i}", [128, 384], F32) for i in range(3)]
    ltap = [PW.alloc(f"ltap{i}", [128, 384], F32) for i in range(3)]
    xin = [PW.alloc(f"xin{i}", [128, 1024], F32) for i in range(4)]
    dump("h2T", h2T[:], [128, 8, L], [("h2T", t) for t in range(16)])
    h2keys = [("h2T", t) for t in range(16)]
    for half in range(2):
        for ii in range(11):
            i = half * 11 + ii
            wload(wua[:, ii], wup_d, i * 128, 128, ("wua", ii))
            wload(wug[:, ii], wup_d, DFF + i * 128, 128, ("wug", ii))
            wload(wdn[:, ii:ii + 1, :], wdn_d, 0, 1024, ("wdn", ii), kcs=1, row0=i * 128)
            S.op("dve", "tensor_tensor", r=[("wdn", ii), ("g2b",)], w=[("wdn", ii)], out=wdn[:, ii, :], in0=wdn[:, ii, :], in1=g2b[:], op=MUL)
        for w in range(6):
            lo = w * 384
            hi = min(L, lo + 384)
            n = hi - lo
            il = max(lo - 1, 0)
            ih = min(hi + 1, L)
            nin = ih - il
            off = lo - il
            h2keys = [("h2T", t) for t in range(il // 128, (ih - 1) // 128 + 1)]
            for ii in range(11):
                i = half * 11 + ii
                ab = ii % 3
                pa, pg = 2 * ab, 2 * ab + 1
                S.mm([("matmul", dict(out=psb[pa][:, 0:nin], lhsT=wua[:, ii, kc, :], rhs=h2T[:, kc, il:ih],
                                      start=(kc == 0), stop=(kc == 7))) for kc in range(8)], r=[("wua", ii)] + h2keys, w=[PSK(pa)])
                S.mm([("matmul", dict(out=psb[pg][:, 0:nin], lhsT=wug[:, ii, kc, :], rhs=h2T[:, kc, il:ih],
                                      start=(kc == 0), stop=(kc == 7))) for kc in range(8)], r=[("wug", ii)] + h2keys, w=[PSK(pg)])
                for (pb_, acc, akey, ch) in ((pa, accA[ab], (f"accA{ab}",), i), (pg, accG[ab], (f"accG{ab}",), 22 + i)):
                    cw = lambda j: vecs[:, V_CW + j * 44 + ch:V_CW + j * 44 + ch + 1]
                    S.op("act", "activation", r=[PSK(pb_), VK], w=[akey], out=acc[:, 0:n], in_=psb[pb_][:, off:off + n], func=AF.Identity,
                         scale=cw(1), bias=vecs[:, V_CB + ch:V_CB + ch + 1])
                    j0 = 1 - off
                    if FFN_SPLIT and pb_ == pa:
                        lk = (f"ltap{ab}",)
                        S.op("act", "activation", r=[PSK(pb_), VK], w=[lk], out=ltap[ab][:, j0:n], in_=psb[pb_][:, off + j0 - 1:off + n - 1],
                             func=AF.Copy, scale=cw(0))
                        S.op("pool", "tensor_tensor", r=[lk, akey], w=[akey], out=acc[:, j0:n], in0=acc[:, j0:n], in1=ltap[ab][:, j0:n], op=ADD)
                    else:
                        S.op("dve", "scalar_tensor_tensor", r=[PSK(pb_), VK, akey], w=[akey], out=acc[:, j0:n],
                             in0=psb[pb_][:, off + j0 - 1:off + n - 1], scalar=cw(0), in1=acc[:, j0:n], op0=MUL, op1=ADD)
                    j1 = min(n, nin - 1 - off)
                    S.op("dve", "scalar_tensor_tensor", r=[PSK(pb_), VK, akey], w=[akey], out=acc[:, 0:j1],
                         in0=psb[pb_][:, off + 1:off + 1 + j1], scalar=cw(2), in1=acc[:, 0:j1], op0=MUL, op1=ADD)
                S.op("act", "activation", r=[(f"accG{ab}",)], w=[(f"accG{ab}",)], out=accG[ab][:, 0:n], in_=accG[ab][:, 0:n], func=AF.Silu)
                S.op("pool", "tensor_tensor", r=[(f"accG{ab}",), (f"accA{ab}",)], w=[("actT", ii)], out=actT[:, ii, 0:n],
                     in0=accG[ab][:, 0:n], in1=accA[ab][:, 0:n], op=MUL)
            for m in range(n // 128):
                tile = lo // 128 + m
                bi = tile % 4
                xk = (f"xin{bi}",)
                src = xmid_d if half == 0 else out_d
                skey = ("xmid", tile) if half == 0 else ("outd", tile)
                S.dma("sp", r=[skey], w=[xk], out=xin[bi][:], in_=src[tile * 128:(tile + 1) * 128, :])
                for nb_ in range(2):
                    bk = 6 + nb_
                    S.mm([("matmul", dict(out=psb[bk][:], lhsT=actT[:, ii, m * 128:(m + 1) * 128], rhs=wdn[:, ii, nb_ * 512:(nb_ + 1) * 512],
                                          start=(ii == 0), stop=(ii == 10))) for ii in range(11)],
                         r=[("actT", ii) for ii in range(11)] + [("wdn", ii) for ii in range(11)], w=[PSK(bk)])
                    S.op("dve", "tensor_tensor", r=[PSK(bk), xk], w=[xk], out=xin[bi][:, nb_ * 512:(nb_ + 1) * 512], in0=psb[bk][:],
                         in1=xin[bi][:, nb_ * 512:(nb_ + 1) * 512], op=ADD)
                S.dma("sp", r=[xk], w=[("outd", tile)], out=out_d[tile * 128:(tile + 1) * 128, :], in_=xin[bi][:])
    finish()


_CACHE = {}


def _run(inputs, debug=False, core_ids=None, stop_after=99):
    key = ("prog", debug, stop_after)
    if key not in _CACHE:
        _CACHE[key] = build_program(debug, stop_after)
    nc, dbg_names, S = _CACHE[key]
    cf, cb, cb2 = host_constants()
    f = lambda a: np.ascontiguousarray(np.asarray(a, dtype=np.float32))
    x, c, ctx, c_ctx = f(inputs["x"]), f(inputs["c"]), f(inputs["ctx"]), f(inputs["c_ctx"])
    shared = {
        "bmod_row": f(inputs["b_mod"]).reshape(1, 6144), "cstf": cf, "cstb": cb, "cstb2": cb2,
        "w_mod": f(inputs["w_mod"])[0], "w_in": f(inputs["w_in"])[0], "w_ba": f(inputs["w_branch_att"])[0],
        "w_bm": f(inputs["w_branch_ml"])[0], "w_out": f(inputs["w_out"])[0], "w_up": f(inputs["w_up"])[0],
        "w_down": f(inputs["w_down"])[0],
    }
    cores = list(range(8)) if core_ids is None else core_ids
    in_maps = []
    for b in cores:
        m = dict(shared)
        m["x"] = x[b]
        m["ctx"] = ctx[b]
        m["vecs"] = host_vecs(b, c, c_ctx, f(inputs["b_mod"]), f(inputs["norm1_w"]), f(inputs["norm2_w"]),
                              f(inputs["ml_norm_w"]), f(inputs["conv_w"]), f(inputs["conv_b"]), f(inputs["q_norm_w"]),
                              f(inputs["k_norm_w"]), f(inputs["ml_gate_b"]), f(inputs["attn_sink"]))
        in_maps.append(m)
    res = run_bass_kernel_spmd(nc, in_maps, core_ids=list(range(len(cores))))
    return res, dbg_names


def kernel(**inputs):
    res, _ = _run(inputs)
    return np.stack([np.asarray(r["out"], dtype=np.float32) for r in res.results], axis=0)
```
